# Optimizing a Trainium2 kernel written in Bass

```python
import math
import jax, jax.numpy as jnp
from jax import lax
import numpy as np

D_MODEL = 1024
BATCH = 16
SEQ = 2048
DEPTH = 1

MIX_WIDTH = D_MODEL
CONV_WIDTH = MIX_WIDTH // 2
CONV_GROUP_SIZE = 64
CONV_GROUPS = CONV_WIDTH // CONV_GROUP_SIZE
SHORT_CONV_K = 3
ATTN_WIDTH = MIX_WIDTH - CONV_WIDTH
ATTN_HALF_DIM = 64
ATTN_V_DIM = 2 * ATTN_HALF_DIM
ATTN_HEADS = ATTN_WIDTH // ATTN_V_DIM
QK_WIDTH = ATTN_HEADS * 2 * ATTN_HALF_DIM
IN_WIDTH = 3 * CONV_WIDTH + 2 * QK_WIDTH + ATTN_WIDTH
D_FF = 256 * ((int(8 * D_MODEL / 3) + 255) // 256)
FFN_CONV_K = 3
ROPE_THETA = 10000.0
NORM_EPS = 1e-6
SUBLN_EPS = 1e-5
Q_BLOCK = 128

kernel_name = "hybrid_shortconv_diffattn_convffn"


def rmsnorm(x, g, eps=NORM_EPS):
    xf = x.astype(jnp.float32)
    y = xf * lax.rsqrt(jnp.mean(xf * xf, axis=-1, keepdims=True) + eps)
    return (y * g.astype(jnp.float32)).astype(x.dtype)


def causal_dwconv(u, w):
    k = w.shape[0]
    s = u.shape[1]
    up = jnp.pad(u, ((0, 0), (k - 1, 0), (0, 0)))
    y = up[:, 0:s] * w[0]
    for j in range(1, k):
        y = y + up[:, j:j + s] * w[j]
    return y


def rope_tables(seq, dim):
    inv = ROPE_THETA ** (-jnp.arange(0, dim, 2, dtype=jnp.float32) / dim)
    ang = jnp.arange(seq, dtype=jnp.float32)[:, None] * inv[None, :]
    return jnp.cos(ang), jnp.sin(ang)


def apply_rope(x, cos, sin):
    half = x.shape[-1] // 2
    xf = x.astype(jnp.float32)
    x1, x2 = xf[..., :half], xf[..., half:]
    c = cos[None, :, None, :]
    s = sin[None, :, None, :]
    return jnp.concatenate([x1 * c - x2 * s, x2 * c + x1 * s], axis=-1).astype(x.dtype)


def diff_attention(q, k, v, lam, subln_g, lambda_init):
    b, s = q.shape[0], q.shape[1]
    q = q * (ATTN_HALF_DIM ** -0.5)
    outs = []
    for i in range(s // Q_BLOCK):
        s0 = i * Q_BLOCK
        e = s0 + Q_BLOCK
        qb = q[:, s0:e]
        kb = k[:, :e]
        vb = v[:, :e]
        sc = jnp.einsum('bqhd,bkhd->bhqk', qb, kb).astype(jnp.float32)
        mask = (s0 + jnp.arange(Q_BLOCK))[:, None] >= jnp.arange(e)[None, :]
        sc = jnp.where(mask[None, None], sc, -jnp.inf)
        p = jax.nn.softmax(sc, axis=-1).reshape(b, ATTN_HEADS, 2, Q_BLOCK, e)
        a = p[:, :, 0] - lam * p[:, :, 1]
        outs.append(jnp.einsum('bhqk,bkhe->bqhe', a.astype(vb.dtype), vb))
    o = jnp.concatenate(outs, axis=1)
    o = rmsnorm(o, subln_g, SUBLN_EPS) * (1.0 - lambda_init)
    return o.reshape(b, s, ATTN_HEADS * ATTN_V_DIM)


def setup_inputs(seed: int = 0) -> dict:
    key = jax.random.key(seed)
    ks = jax.random.split(key, 18)
    f32 = jnp.float32
    nrm = lambda k, shape, scale: jax.random.normal(k, shape, f32) * scale
    return {
        "x": nrm(ks[0], (BATCH, SEQ, D_MODEL), 1.0),
        "norm1_g": 1.0 + nrm(ks[1], (DEPTH, D_MODEL), 0.02),
        "w_in": nrm(ks[2], (DEPTH, D_MODEL, IN_WIDTH), D_MODEL ** -0.5),
        "short_conv_w": nrm(ks[3], (DEPTH, SHORT_CONV_K, CONV_WIDTH), SHORT_CONV_K ** -0.5),
        "mix_a_norm_g": 1.0 + nrm(ks[4], (DEPTH, CONV_WIDTH), 0.02),
        "lambda_q1": nrm(ks[5], (DEPTH, ATTN_HALF_DIM), 0.1),
        "lambda_k1": nrm(ks[6], (DEPTH, ATTN_HALF_DIM), 0.1),
        "lambda_q2": nrm(ks[7], (DEPTH, ATTN_HALF_DIM), 0.1),
        "lambda_k2": nrm(ks[8], (DEPTH, ATTN_HALF_DIM), 0.1),
        "subln_g": 1.0 + nrm(ks[9], (DEPTH, ATTN_V_DIM), 0.02),
        "w_out": nrm(ks[10], (DEPTH, MIX_WIDTH, D_MODEL), MIX_WIDTH ** -0.5),
        "norm2_g": 1.0 + nrm(ks[11], (DEPTH, D_MODEL), 0.02),
        "w_up": nrm(ks[12], (DEPTH, D_MODEL, 2 * D_FF), D_MODEL ** -0.5),
        "ffn_conv_w": nrm(ks[13], (DEPTH, FFN_CONV_K, 2 * D_FF), FFN_CONV_K ** -0.5),
        "ffn_conv_b": nrm(ks[14], (DEPTH, 2 * D_FF), 0.01),
        "w_down": nrm(ks[15], (DEPTH, D_FF, D_MODEL), D_FF ** -0.5),
        "final_g": 1.0 + nrm(ks[16], (D_MODEL,), 0.02),
    }


def reference(x, norm1_g, w_in, short_conv_w, mix_a_norm_g, lambda_q1, lambda_k1,
              lambda_q2, lambda_k2, subln_g, w_out, norm2_g, w_up, ffn_conv_w,
              ffn_conv_b, w_down, final_g):
    b, s, _ = x.shape
    cos, sin = rope_tables(s, ATTN_HALF_DIM)
    c1 = CONV_WIDTH
    c2 = 2 * CONV_WIDTH
    c3 = 3 * CONV_WIDTH
    c4 = c3 + QK_WIDTH
    c5 = c4 + QK_WIDTH
    for l in range(DEPTH):
        lambda_init = 0.8 - 0.6 * math.exp(-0.3 * l)
        h = rmsnorm(x, norm1_g[l])
        proj = h @ w_in[l]
        gate_b = proj[..., 0:c1]
        gate_c = proj[..., c1:c2]
        xa = proj[..., c2:c3]
        q = proj[..., c3:c4].reshape(b, s, 2 * ATTN_HEADS, ATTN_HALF_DIM)
        k = proj[..., c4:c5].reshape(b, s, 2 * ATTN_HEADS, ATTN_HALF_DIM)
        v = proj[..., c5:].reshape(b, s, ATTN_HEADS, ATTN_V_DIM)
        ya = gate_b * causal_dwconv(gate_c * xa, short_conv_w[l])
        ya = rmsnorm(ya, mix_a_norm_g[l])
        q = apply_rope(q, cos, sin)
        k = apply_rope(k, cos, sin)
        lam = (jnp.exp(jnp.sum(lambda_q1[l].astype(jnp.float32) * lambda_k1[l].astype(jnp.float32)))
               - jnp.exp(jnp.sum(lambda_q2[l].astype(jnp.float32) * lambda_k2[l].astype(jnp.float32)))
               + lambda_init)
        yb = diff_attention(q, k, v, lam, subln_g[l], lambda_init)
        x = x + jnp.concatenate([ya, yb], axis=-1) @ w_out[l]
        h = rmsnorm(x, norm2_g[l])
        up = causal_dwconv(h @ w_up[l], ffn_conv_w[l]) + ffn_conv_b[l]
        g, u = up[..., :D_FF], up[..., D_FF:]
        x = x + (jax.nn.silu(g) * u) @ w_down[l]
    return rmsnorm(x, final_g)
```

```python
import numpy as np
import concourse.bass as bass
import concourse.mybir as mybir
from concourse.bass_utils import run_bass_kernel_spmd

F32 = mybir.dt.float32
BF16 = mybir.dt.bfloat16
AF = mybir.ActivationFunctionType
ALU = mybir.AluOpType
AX = mybir.AxisListType

NCORES = 8
D = 1024
S = 2048
NSEQ = 2
TOK = NSEQ * S
DFF = 2816
NFF = DFF // 128
SBUF_BASE = 16512
SBUF_LIMIT = 229344
NCOLS = 193


class Buf:
    __slots__ = ("name", "w", "r")

    def __init__(self, name):
        self.name = name
        self.w = None
        self.r = {}


class Sync:
    def __init__(self, nc, es):
        self.nc = nc
        self.es = es
        self.engs = {"pe": nc.tensor, "act": nc.scalar, "dve": nc.vector, "pool": nc.gpsimd, "sp": nc.sync}
        self.semh = {}
        self.cnt = {}
        for e in self.engs:
            self.semh[e] = es.enter_context(nc.semaphore("sem_" + e))
            self.cnt[e] = 0
        self.seen = {e: {} for e in self.engs}
        self.bufs = []
        self.dma_keys = []
        self.shared_keys = set()

    def buf(self, name):
        b = Buf(name)
        self.bufs.append(b)
        return b

    def dma_sem(self, name):
        key = "d_" + name
        self.semh[key] = self.es.enter_context(self.nc.semaphore("sem_" + key))
        self.cnt[key] = 0
        self.dma_keys.append(key)
        return key

    def _waits(self, e, reads, writes, same_ok, strict=()):
        waits = {}
        for b in strict:
            if b.w is not None and waits.get(b.w[0], 0) < b.w[1]:
                waits[b.w[0]] = b.w[1]

        def need(ev):
            if ev is None:
                return
            k, v = ev
            if same_ok and k == e:
                return
            if waits.get(k, 0) < v:
                waits[k] = v

        for b in reads:
            need(b.w)
        for b in writes:
            need(b.w)
            for k, v in b.r.items():
                need((k, v))
        eng = self.engs[e]
        for k, v in waits.items():
            if self.seen[e].get(k, 0) >= v:
                continue
            eng.wait_ge(self.semh[k], v)
            self.seen[e][k] = v

    def op(self, e, fn, reads=(), writes=(), inc=True, strict=()):
        self._waits(e, reads, writes, same_ok=True, strict=strict)
        ins = fn(self.engs[e])
        if inc:
            self.cnt[e] += 1
            ins.then_inc(self.semh[e], 1)
            v = self.cnt[e]
        else:
            v = self.cnt[e] + 1
        for b in reads:
            if b.r.get(e, 0) < v:
                b.r[e] = v
        for b in writes:
            b.w = (e, v)
            b.r = {}

    def dma(self, q, key, out, in_, reads=(), writes=()):
        self._waits(q, reads, writes, same_ok=False)
        if self.cnt[key] and self.seen[q].get(key, 0) < self.cnt[key] and key not in self.shared_keys:
            self.engs[q].wait_ge(self.semh[key], self.cnt[key])
            self.seen[q][key] = self.cnt[key]
        ins = self.engs[q].dma_start(out=out, in_=in_)
        self.cnt[key] += 16
        ins.then_inc(self.semh[key], 16)
        v = self.cnt[key]
        for b in reads:
            if b.r.get(key, 0) < v:
                b.r[key] = v
        for b in writes:
            b.w = (key, v)
            b.r = {}

    def barrier(self):
        keys = list(self.engs.keys()) + self.dma_keys
        for e in self.engs:
            for k in keys:
                if k == e:
                    continue
                v = self.cnt[k]
                if v == 0 or self.seen[e].get(k, 0) >= v:
                    continue
                self.engs[e].wait_ge(self.semh[k], v)
                self.seen[e][k] = v
        for b in self.bufs:
            b.w = None
            b.r = {}

    def final_wait(self, e, keys):
        for k in keys:
            v = self.cnt[k]
            if v:
                self.engs[e].wait_ge(self.semh[k], v)


class Arena:
    def __init__(self, nc):
        self.nc = nc
        self.p = SBUF_BASE
        self.n = 0

    def alloc(self, name, shape, dtype):
        esz = 2 if dtype == BF16 else 4
        per = esz
        for d in shape[1:]:
            per *= d
        off = (self.p + 31) // 32 * 32
        assert off + per <= SBUF_LIMIT, (name, off, per)
        self.n += 1
        t = self.nc.alloc_sbuf_tensor_at(f"{name}_{self.n}", list(shape), dtype, offset=off)
        self.p = off + per
        return t


def build_nc(debug=False):
    from contextlib import ExitStack

    nc = bass.Bass("TRN2", target_bir_lowering=False)
    dt = nc.dram_tensor
    x_d = dt("x", [TOK, D], F32, kind="ExternalInput").ap()
    win_d = dt("w_in_fm", [20 * 128, 1024], F32, kind="ExternalInput").ap()
    wv_d = dt("w_v", [1024, 512], F32, kind="ExternalInput").ap()
    wout_d = dt("w_out", [1024, 1024], F32, kind="ExternalInput").ap()
    wup_d = dt("w_up_fm", [NFF * 128 * 2, 1024], F32, kind="ExternalInput").ap()
    wdn_d = dt("w_down", [DFF, 1024], F32, kind="ExternalInput").ap()
    g3_d = dt("g3", [3, 1024], F32, kind="ExternalInput").ap()
    lam_d = dt("lamv", [1, 256], F32, kind="ExternalInput").ap()
    cols_d = dt("cols", [128, NCOLS], F32, kind="ExternalInput").ap()
    cs_d = dt("cs", [2 * 128, S], F32, kind="ExternalInput").ap()
    cm_d = dt("cmat", [4 * 128, 128], F32, kind="ExternalInput").ap()
    out_d = dt("out", [TOK, D], F32, kind="ExternalOutput").ap()
    winb_d = dt("w_in_b", [20 * 128, 1024], BF16, kind="Internal").ap()
    wvb_d = dt("w_v_b", [1024, 512], BF16, kind="Internal").ap()
    woutb_d = dt("w_out_b", [1024, 1024], BF16, kind="Internal").ap()
    wupb_d = dt("w_up_b", [NFF * 128 * 2, 1024], BF16, kind="Internal").ap()
    wdnb_d = dt("w_down_b", [DFF, 1024], BF16, kind="Internal").ap()

    if debug:
        dbg_mix = dt("dbg_mix", [128, 8 * S], BF16, kind="ExternalOutput").ap()
        dbg_q = dt("dbg_q", [128, 4 * S], BF16, kind="ExternalOutput").ap()
        dbg_k = dt("dbg_k", [128, 4 * S], BF16, kind="ExternalOutput").ap()
        dbg_v = dt("dbg_v", [128, 16 * 512], BF16, kind="ExternalOutput").ap()
    es = ExitStack()
    sy = Sync(nc, es)
    ar = Arena(nc)

    pb = [nc.alloc_psum_tensor(f"pb{i}", [128, 512], F32) for i in range(8)]
    pbT = pb[7][:, :].bitcast(BF16)
    pbB = [sy.buf(f"pb{i}") for i in range(8)]
    pbTB = pbB[7]

    mixT = ar.alloc("mixT", [128, 8, S], BF16)
    wout = ar.alloc("wout", [128, 8, 1024], BF16)
    cols = ar.alloc("cols", [128, NCOLS], F32)
    cmf = ar.alloc("cmf", [128, 4, 128], F32)
    ident = ar.alloc("ident", [128, 128], BF16)
    pswap = ar.alloc("pswap", [128, 128], BF16)
    tri = ar.alloc("tri", [128, 128], BF16)
    ones_b = ar.alloc("ones_b", [128, 128], BF16)
    lamt = ar.alloc("lamt", [128, 256], F32)
    sm = ar.alloc("sm", [128, 16], F32)
    hal = ar.alloc("hal", [128, 2, 2 * NFF, 2], F32)
    stat = ar.alloc("stat", [128, 8, 4], F32)
    region = ar.p

    mixB = [[sy.buf(f"mix{c}_{t}") for t in range(4)] for c in range(8)]
    woutB = sy.buf("wout")
    constB = sy.buf("const")
    smB = sy.buf("sm")
    halB = [[sy.buf(f"hal{p}_{c}") for c in range(2 * NFF)] for p in range(2)]
    statB = [sy.buf(f"stat{i}") for i in range(8)]

    winbB, wvbB, woutbB, wupbB, wdnbB = (sy.buf(n) for n in ("winb", "wvb", "woutb", "wupb", "wdnb"))

    k_c1 = sy.dma_sem("cast_win")
    k_c2 = sy.dma_sem("cast_wv")
    k_c3 = sy.dma_sem("cast_wout")
    k_c4 = sy.dma_sem("cast_wup")
    k_c5 = sy.dma_sem("cast_wdn")
    sy.dma("pool", k_c2, wvb_d[:, :], wv_d[:, :], writes=[wvbB])
    sy.dma("pool", k_c1, winb_d[:, :], win_d[:, :], writes=[winbB])

    k_const = sy.dma_sem("const")
    sy.shared_keys.add(k_const)
    sy.dma("sp", k_const, cols[:], cols_d[:, :])
    sy.dma("sp", k_const, cmf[:], cm_d.rearrange("(m p) c -> p m c", p=128))
    sy.dma("sp", k_const, lamt[:], lam_d[0:1, :].partition_broadcast(128))
    constB.w = (k_const, sy.cnt[k_const])
    k_wout = sy.dma_sem("wout")

    V = lambda f, **kw: sy.op("dve", f, **kw)
    A = lambda f, **kw: sy.op("act", f, **kw)
    P = lambda f, **kw: sy.op("pool", f, **kw)
    T = lambda f, **kw: sy.op("pe", f, **kw)

    V(lambda e: e.tensor_copy(out=ident[:], in_=cmf[:, 0, :]), reads=[constB], writes=[smB])
    V(lambda e: e.tensor_copy(out=pswap[:], in_=cmf[:, 1, :]), reads=[constB], writes=[smB])
    V(lambda e: e.tensor_copy(out=tri[:], in_=cmf[:, 2, :]), reads=[constB], writes=[smB])
    V(lambda e: e.tensor_copy(out=ones_b[:], in_=cmf[:, 3, :]), reads=[constB], writes=[smB])
    V(lambda e: e.memset(sm[:, 0:1], 1e-6), writes=[smB])
    V(lambda e: e.memset(sm[:, 1:2], 1e-5), writes=[smB])
    V(lambda e: e.memset(hal[:], 0.0), writes=halB[0] + halB[1])
    lt = ar.alloc("lamtmp", [128, 128], F32)
    V(lambda e: e.tensor_tensor(out=lt[:, 0:64], in0=lamt[:, 0:64], in1=lamt[:, 64:128], op=ALU.mult), reads=[constB], writes=[smB])
    V(lambda e: e.tensor_tensor(out=lt[:, 64:128], in0=lamt[:, 128:192], in1=lamt[:, 192:256], op=ALU.mult), writes=[smB])
    V(lambda e: e.reduce_sum(out=sm[:, 2:3], in_=lt[:, 0:64], axis=AX.X), writes=[smB], strict=[smB])
    V(lambda e: e.reduce_sum(out=sm[:, 3:4], in_=lt[:, 64:128], axis=AX.X), writes=[smB], strict=[smB])
    A(lambda e: e.activation(out=sm[:, 4:6], in_=sm[:, 2:4], func=AF.Exp), reads=[smB], writes=[smB])
    V(lambda e: e.tensor_sub(out=sm[:, 6:7], in0=sm[:, 5:6], in1=sm[:, 4:5]), reads=[smB], writes=[smB])
    V(lambda e: e.tensor_scalar_add(out=sm[:, 6:7], in0=sm[:, 6:7], scalar1=-0.2), writes=[smB], strict=[smB])
    V(lambda e: e.tensor_scalar_mul(out=sm[:, 7:8], in0=cols[:, 192:193], scalar1=0.8), reads=[constB], writes=[smB])
    region = ar.p

    CW = lambda j, c: cols[:, j * 44 + c:j * 44 + c + 1]
    CB = lambda c: cols[:, 132 + c:133 + c]
    SCW = lambda j, i: cols[:, 176 + j * 4 + i:177 + j * 4 + i]
    GA = lambda i: cols[:, 188 + i:189 + i]

    rot = {"i": 0}

    def next_bank(n=6):
        i = rot["i"] % n
        rot["i"] += 1
        return i

    def rms_stat(slot, src_ap, junk_ap, srcB, junkB, n_feat, eps_col):
        sb = statB[slot]
        A(lambda e: e.activation(out=junk_ap, in_=src_ap, func=AF.Square, accum_out=stat[:, slot, 0:1]),
          reads=srcB, writes=[junkB, sb])
        A(lambda e: e.activation(out=stat[:, slot, 1:2], in_=stat[:, slot, 0:1], func=AF.Sqrt,
                                 scale=1.0 / n_feat, bias=sm[:, eps_col:eps_col + 1]), reads=[smB], writes=[sb], strict=[sb])
        V(lambda e: e.reciprocal(out=stat[:, slot, 2:3], in_=stat[:, slot, 1:2]), reads=[sb], writes=[sb])
        return stat[:, slot, 2:3], sb

    out_keys = []

    for seq in range(NSEQ):
        if seq > 0:
            sy.barrier()
        ar.p = region
        qT = ar.alloc("qT", [128, 4, S], BF16)
        kT = ar.alloc("kT", [128, 4, S], BF16)
        vt = ar.alloc("vt", [128, 16, 512], BF16)
        hT = ar.alloc("hT", [128, 8, S], BF16)
        p2_base = ar.p - 32768
        p01_base = ar.p
        wv = ar.alloc("wv", [128, 8, 512], BF16)
        g1bc = ar.alloc("g1bc", [128, 1024], F32)
        xt = [ar.alloc(f"xt{i}", [128, 1024], F32) for i in range(4)]
        hb = [ar.alloc(f"hb{i}", [128, 1024], BF16) for i in range(2)]
        ar_save = ar.p
        ar.p = p01_base + 65536
        wsl = [ar.alloc(f"wsl{i}", [128, 8, 128], BF16) for i in range(3)]
        wsl_end = ar.p
        ar.p = ar_save
        wslB = [sy.buf(f"wsl{i}") for i in range(3)]
        k_wsl = [sy.dma_sem(f"wsl{seq}_{i}") for i in range(3)]

        def load_win(ci):
            s3 = ci % 3
            sy.dma("sp", k_wsl[s3], wsl[s3][:],
                   winb_d[ci * 128:(ci + 1) * 128, :].rearrange("p (k c) -> p k c", k=8),
                   reads=[winbB], writes=[wslB[s3]])
        qB = [[sy.buf(f"q{h}_{t}") for t in range(4)] for h in range(4)]
        kB = [[sy.buf(f"k{h}_{t}") for t in range(4)] for h in range(4)]
        vB = [sy.buf(f"v{t}") for t in range(16)]
        hTB = [sy.buf(f"hT{t}") for t in range(16)]
        wvB, g1B = sy.buf("wv"), sy.buf("g1")
        xtB = [sy.buf(f"xt{i}") for i in range(4)]
        hbB = [sy.buf(f"hb{i}") for i in range(2)]
        k_xt = [sy.dma_sem(f"xt{seq}_{i}") for i in range(4)]
        k_m0 = sy.dma_sem(f"m0_{seq}")
        k_m1 = sy.dma_sem(f"m1_{seq}")

        sy.dma("sp", k_m0, g1bc[:], g3_d[0:1, :].partition_broadcast(128), writes=[g1B])

        def load_x(i):
            r0 = seq * S + i * 128
            sy.dma("sp", k_xt[i % 4], xt[i % 4][:], x_d[r0:r0 + 128, :], writes=[xtB[i % 4]])

        junk0 = ar.alloc("junk0", [128, 1024], BF16)
        junk0B = sy.buf("junk0")
        for i0 in range(4):
            load_x(i0)
        sy.dma("sp", k_m1, wv[:], wvb_d.rearrange("(c p) n -> p c n", p=128), reads=[wvbB], writes=[wvB])
        st = {0: rms_stat(0, xt[0][:], junk0[:], [xtB[0]], junk0B, D, 0)}
        for i in range(16):
            sl = i % 2
            x4 = i % 4
            rstd, sb = st[i]
            V(lambda e: e.scalar_tensor_tensor(out=hb[sl][:], in0=xt[x4][:], scalar=rstd, in1=g1bc[:],
                                               op0=ALU.mult, op1=ALU.mult),
              reads=[xtB[x4], sb, g1B], writes=[hbB[sl]], strict=[sb])
            if i + 4 < 16:
                load_x(i + 4)
            if i + 1 < 16:
                st[i + 1] = rms_stat((i + 1) % 2, xt[(i + 1) % 4][:], junk0[:], [xtB[(i + 1) % 4]], junk0B, D, 0)
            for kc in range(8):
                T(lambda e: e.transpose(out=pbT[:, kc * 128:(kc + 1) * 128], in_=hb[sl][:, kc * 128:(kc + 1) * 128],
                                        identity=ident[:]),
                  reads=[hbB[sl], smB], writes=[pbTB], inc=(kc == 7))
            A(lambda e: e.activation(out=hT[:, :, i * 128:(i + 1) * 128],
                                     in_=pbT[:].rearrange("p (k c) -> p k c", k=8), func=AF.Copy),
              reads=[pbTB], writes=[hTB[i]])

            def vproj(i=i):
                bi = next_bank(2)
                for kc in range(8):
                    T(lambda e: e.matmul(pb[bi][:, :], lhsT=hT[:, kc, i * 128:(i + 1) * 128], rhs=wv[:, kc, :],
                                         start=(kc == 0), stop=(kc == 7)),
                      reads=[hTB[i], wvB], writes=[pbB[bi]], inc=(kc == 7))
                A(lambda e: e.activation(out=vt[:, i, :], in_=pb[bi][:, :], func=AF.Copy), reads=[pbB[bi]], writes=[vB[i]])
            if i > 0:
                vproj(i - 1)
            if i == 15:
                vproj(15)

        load_win(0)
        load_win(1)
        sy.barrier()
        ar.p = p01_base
        cs = ar.alloc("cs", [128, 2, S], F32)
        gcs = [ar.alloc(f"gcs{i}", [128, 512], F32) for i in range(4)]
        cx = [ar.alloc(f"cx{i}", [128, 516], F32) for i in range(2)]
        yv = [ar.alloc(f"yv{i}", [128, 512], F32) for i in range(4)]
        yaf = [ar.alloc(f"yaf{i}", [128, 512], F32) for i in range(2)]
        sqt = [ar.alloc(f"sqt{i}", [128, 512], F32) for i in range(1)] * 2
        accss = ar.alloc("accss", [128, S], F32)
        stdt = [ar.alloc(f"stdt{i}", [128, 512], F32) for i in range(1)] * 2
        qraw = [ar.alloc(f"qraw{i}", [128, 512], BF16) for i in range(2)]
        rt = [ar.alloc(f"rt{i}", [128, 512], F32) for i in range(2)]
        ru = [ar.alloc(f"ru{i}", [128, 512], F32) for i in range(2)]
        assert ar.p <= p01_base + 65536
        p1_end = wsl_end
        csB = sy.buf("cs")
        gcsB = [sy.buf(f"gcs{i}") for i in range(4)]
        cxB = [sy.buf(f"cx{i}") for i in range(2)]
        yvB = [sy.buf(f"yv{i}") for i in range(4)]
        yafB = [sy.buf(f"yaf{i}") for i in range(2)]
        sqtB = [sy.buf(f"sqt{i}") for i in range(1)] * 2
        accB = [sy.buf(f"acc{i}") for i in range(4)]
        stdB = [sy.buf(f"std{i}") for i in range(1)] * 2
        qrawB = [sy.buf(f"qraw{i}") for i in range(2)]
        rtB = [sy.buf(f"rt{i}") for i in range(2)]
        ruB = [sy.buf(f"ru{i}") for i in range(2)]
        k_cs = sy.dma_sem(f"cs{seq}")

        if seq == 0:
            sy.dma("pool", k_c3, woutb_d[:, :], wout_d[:, :], writes=[woutbB])
        sy.dma("sp", k_cs, cs[:], cs_d.rearrange("(m p) t -> p m t", p=128), writes=[csB])
        V(lambda e: e.memset(cx[0][:, 0:2], 0.0), writes=[cxB[0]])
        V(lambda e: e.memset(cx[1][:, 0:2], 0.0), writes=[cxB[1]])

        deferred = []
        cnt = {"gc": 0, "cx": 0, "ya": 0, "qr": 0}
        for ci in range(20):
            if ci + 2 < 20:
                load_win(ci + 2)
            s3 = ci % 3
            for tt in range(4):
                bi = next_bank(6)
                c0 = tt * 512
                for kc in range(8):
                    T(lambda e: e.matmul(pb[bi][:, :], lhsT=wsl[s3][:, kc, :], rhs=hT[:, kc, c0:c0 + 512],
                                         start=(kc == 0), stop=(kc == 7)),
                      reads=[wslB[s3]], writes=[pbB[bi]], inc=(kc == 7))
                for f in deferred:
                    f()
                deferred = []
                if ci < 12:
                    i, typ = ci // 3, ci % 3
                    if typ == 0:
                        g = tt
                        A(lambda e: e.activation(out=gcs[g][:], in_=pb[bi][:, :], func=AF.Copy),
                          reads=[pbB[bi]], writes=[gcsB[g]])
                    elif typ == 1:
                        g = tt
                        c = cnt["cx"] % 2
                        cnt["cx"] += 1
                        if tt == 0:
                            P(lambda e: e.memset(cx[c][:, 0:2], 0.0), writes=[cxB[c]])
                        V(lambda e: e.tensor_tensor(out=cx[c][:, 2:514], in0=pb[bi][:, :], in1=gcs[g][:], op=ALU.mult),
                          reads=[pbB[bi], gcsB[g]], writes=[cxB[c]])
                        if tt < 3:
                            P(lambda e: e.tensor_copy(out=cx[1 - c][:, 0:2], in_=cx[c][:, 512:514]),
                              reads=[cxB[c]], writes=[cxB[1 - c]])
                        A(lambda e: e.activation(out=yv[tt][:], in_=cx[c][:, 2:514], func=AF.Copy, scale=SCW(2, i)),
                          reads=[cxB[c], constB], writes=[yvB[tt]])
                        V(lambda e: e.scalar_tensor_tensor(out=yv[tt][:], in0=cx[c][:, 1:513], scalar=SCW(1, i),
                                                           in1=yv[tt][:], op0=ALU.mult, op1=ALU.add),
                          reads=[cxB[c]], writes=[yvB[tt]])
                        V(lambda e: e.scalar_tensor_tensor(out=yv[tt][:], in0=cx[c][:, 0:512], scalar=SCW(0, i),
                                                           in1=yv[tt][:], op0=ALU.mult, op1=ALU.add),
                          reads=[cxB[c]], writes=[yvB[tt]])
                    else:
                        a = cnt["ya"] % 2
                        cnt["ya"] += 1
                        V(lambda e: e.tensor_tensor(out=yaf[a][:], in0=pb[bi][:, :], in1=yv[tt][:], op=ALU.mult),
                          reads=[pbB[bi], yvB[tt]], writes=[yafB[a]])
                        A(lambda e: e.activation(out=mixT[:, i, c0:c0 + 512], in_=yaf[a][:], func=AF.Copy),
                          reads=[yafB[a]], writes=[mixB[i][tt]])
                        if i == 0:
                            A(lambda e: e.activation(out=accss[:, c0:c0 + 512], in_=yaf[a][:], func=AF.Square),
                              reads=[yafB[a]], writes=[accB[tt]])
                        else:
                            A(lambda e: e.activation(out=sqt[a][:], in_=yaf[a][:], func=AF.Square),
                              reads=[yafB[a]], writes=[sqtB[a]])
                            P(lambda e: e.tensor_tensor(out=accss[:, c0:c0 + 512], in0=accss[:, c0:c0 + 512],
                                                        in1=sqt[a][:], op=ALU.add),
                              reads=[sqtB[a]], writes=[accB[tt]])
                else:
                    h = (ci - 12) % 4
                    dstT, dstB = (qT, qB) if ci < 16 else (kT, kB)
                    r = cnt["qr"] % 2
                    cnt["qr"] += 1
                    A(lambda e: e.activation(out=qraw[r][:], in_=pb[bi][:, :], func=AF.Copy),
                      reads=[pbB[bi]], writes=[qrawB[r]])
                    V(lambda e: e.tensor_tensor(out=rt[r][:], in0=pb[bi][:, :], in1=cs[:, 0, c0:c0 + 512], op=ALU.mult),
                      reads=[pbB[bi], csB], writes=[rtB[r]])

                    def rope_tail(r=r, h=h, tt=tt, c0=c0, dstT=dstT, dstB=dstB):
                        b2 = next_bank(6)
                        T(lambda e: e.matmul(pb[b2][:, :], lhsT=pswap[:], rhs=qraw[r][:], start=True, stop=True),
                          reads=[qrawB[r], smB], writes=[pbB[b2]])
                        V(lambda e: e.tensor_tensor(out=ru[r][:], in0=pb[b2][:, :], in1=cs[:, 1, c0:c0 + 512], op=ALU.mult),
                          reads=[pbB[b2], csB], writes=[ruB[r]])
                        P(lambda e: e.tensor_tensor(out=dstT[:, h, c0:c0 + 512], in0=rt[r][:], in1=ru[r][:], op=ALU.add),
                          reads=[rtB[r], ruB[r]], writes=[dstB[h][tt]])

                    deferred.append(rope_tail)
            if ci == 11:
                def ga_piece(tt, bank):
                    c0 = tt * 512
                    T(lambda e: e.matmul(pb[bank][:, :], lhsT=cmf[:, 3, :], rhs=accss[:, c0:c0 + 512], start=True, stop=True),
                      reads=[constB], writes=[pbB[bank]])
                    A(lambda e: e.activation(out=stdt[0][:], in_=pb[bank][:, :], func=AF.Ln, scale=1.0 / 512,
                                             bias=sm[:, 0:1]), reads=[pbB[bank], smB], writes=[stdB[0]])
                    A(lambda e: e.activation(out=stdt[0][:], in_=stdt[0][:], func=AF.Exp, scale=-0.5), writes=[stdB[0]])
                    for i in range(4):
                        V(lambda e: e.scalar_tensor_tensor(out=mixT[:, i, c0:c0 + 512], in0=mixT[:, i, c0:c0 + 512],
                                                           scalar=GA(i), in1=stdt[0][:], op0=ALU.mult, op1=ALU.mult),
                          reads=[stdB[0], constB], writes=[mixB[i][tt]])
        for f in deferred:
            f()
        deferred = []

        sy.barrier()
        ar.p = p2_base
        eT = [ar.alloc(f"eT{i}", [128, 512], BF16) for i in range(6)]
        ssb = [[ar.alloc(f"ssb{g}{j}", [128, 512], F32) for j in range(2)] for g in range(2)]
        osb = [[ar.alloc(f"osb{g}{j}", [128, 512], F32) for j in range(2)] for g in range(2)]
        ot = [ar.alloc(f"ot{i}", [128, 512], F32) for i in range(2)]
        osq = [ar.alloc(f"osq{i}", [128, 512], F32) for i in range(2)]
        assert ar.p <= p01_base
        eTB = [sy.buf(f"eT{i}") for i in range(6)]
        ssbB = [[sy.buf(f"ssb{g}{j}") for j in range(2)] for g in range(2)]
        osbB = [[sy.buf(f"osb{g}{j}") for j in range(2)] for g in range(2)]
        otB = [sy.buf(f"ot{i}") for i in range(2)]
        osqB = [sy.buf(f"osq{i}") for i in range(2)]
        if seq == 0:
            sy.dma("pool", k_c4, wupb_d[:, :], wup_d[:, :], writes=[wupbB])
            sy.dma("pool", k_c5, wdnb_d[:, :], wdn_d[:, :], writes=[wdnbB])
            sy.dma("sp", k_wout, wout[:], woutb_d.rearrange("(c p) n -> p c n", p=128), reads=[woutbB], writes=[woutB])
        ecnt = 0
        scnt = 0
        gcnt = 0
        O = [0, 1]
        Sm = [2, 3]
        fin_pending = []
        for h in range(4):
            for qt in range(4):
                nk = 4 * (qt + 1)
                q0 = qt * 512
                pend = []
                for kt in range(nk):
                    di = kt - 4 * qt
                    c0 = 128 * di if di > 0 else 0
                    cur = []
                    for j in range(2):
                        sbk = 4 + (scnt % 4)
                        scnt += 1
                        es_ = ecnt % 6
                        ecnt += 1
                        pr = slice(64 * j, 64 * j + 64)
                        T(lambda e: e.matmul(pb[sbk][:, c0:512], lhsT=kT[pr, h, kt * 128:(kt + 1) * 128],
                                             rhs=qT[pr, h, q0 + c0:q0 + 512], start=True, stop=True),
                          reads=[kB[h][kt // 4], qB[h][qt]], writes=[pbB[sbk]])
                        A(lambda e: e.activation(out=eT[es_][:, c0:512], in_=pb[sbk][:, c0:512], func=AF.Exp, scale=0.125),
                          reads=[pbB[sbk]], writes=[eTB[es_]])
                        if di >= 0:
                            P(lambda e: e.tensor_tensor(out=eT[es_][:, c0:c0 + 128], in0=eT[es_][:, c0:c0 + 128],
                                                        in1=tri[:], op=ALU.mult), reads=[smB], writes=[eTB[es_]])

                        def av(es_=es_, j=j, kt=kt, c0=c0, nk=nk, h=h):
                            T(lambda e: e.matmul(pb[O[j]][:, c0:512], lhsT=vt[:, kt, h * 128:(h + 1) * 128],
                                                 rhs=eT[es_][:, c0:512], start=(kt == 0), stop=(kt == nk - 1)),
                              reads=[eTB[es_], vB[kt]], writes=[pbB[O[j]]], inc=(kt == nk - 1))
                            T(lambda e: e.matmul(pb[Sm[j]][:, c0:512], lhsT=ones_b[:], rhs=eT[es_][:, c0:512],
                                                 start=(kt == 0), stop=(kt == nk - 1)),
                              reads=[eTB[es_], smB], writes=[pbB[Sm[j]]], inc=True)
                        cur.append(av)
                    for f in pend:
                        f()
                    pend = cur
                    if kt == min(5, nk - 1):
                        for fa, fb in fin_pending:
                            fa()
                    if kt == min(9, nk - 1):
                        for fa, fb in fin_pending:
                            fb()
                        fin_pending = []
                for f in pend:
                    f()
                pend = []
                g = gcnt % 2
                gcnt += 1
                V(lambda e: e.tensor_copy(out=osb[g][0][:], in_=pb[O[0]][:, :]), reads=[pbB[O[0]]], writes=[osbB[g][0]])
                V(lambda e: e.tensor_scalar_mul(out=ssb[g][0][:], in0=pb[Sm[0]][:, :], scalar1=2.0 ** -10), reads=[pbB[Sm[0]]], writes=[ssbB[g][0]])
                V(lambda e: e.tensor_copy(out=osb[g][1][:], in_=pb[O[1]][:, :]), reads=[pbB[O[1]]], writes=[osbB[g][1]])
                V(lambda e: e.tensor_scalar_mul(out=ssb[g][1][:], in0=pb[Sm[1]][:, :], scalar1=2.0 ** -10), reads=[pbB[Sm[1]]], writes=[ssbB[g][1]])
                V(lambda e: e.tensor_tensor(out=ot[g][:], in0=osb[g][0][:], in1=ssb[g][1][:], op=ALU.mult),
                  reads=[osbB[g][0], ssbB[g][1]], writes=[otB[g]])
                V(lambda e: e.tensor_tensor(out=osb[g][1][:], in0=osb[g][1][:], in1=ssb[g][0][:], op=ALU.mult),
                  reads=[ssbB[g][0]], writes=[osbB[g][1]])
                V(lambda e: e.scalar_tensor_tensor(out=ot[g][:], in0=osb[g][1][:], scalar=sm[:, 6:7], in1=ot[g][:],
                                                   op0=ALU.mult, op1=ALU.add), reads=[smB, osbB[g][1]], writes=[otB[g]])
                V(lambda e: e.tensor_tensor(out=ssb[g][0][:], in0=ssb[g][0][:], in1=ssb[g][1][:], op=ALU.mult),
                  reads=[ssbB[g][1]], writes=[ssbB[g][0]])
                V(lambda e: e.scalar_tensor_tensor(out=ssb[g][1][:], in0=ssb[g][0][:], scalar=1e-5, in1=ssb[g][0][:],
                                                   op0=ALU.mult, op1=ALU.mult), reads=[ssbB[g][0]], writes=[ssbB[g][1]])

                def fin_a(g=g):
                    nonlocal scnt
                    V(lambda e: e.tensor_tensor(out=osq[g][:], in0=ot[g][:], in1=ot[g][:], op=ALU.mult), reads=[otB[g]], writes=[osqB[g]])
                    sbk = 4 + (scnt % 4)
                    scnt += 1
                    T(lambda e: e.matmul(pb[sbk][:, :], lhsT=cmf[:, 3, :], rhs=osq[g][:], start=True, stop=True),
                      reads=[osqB[g], constB], writes=[pbB[sbk]])
                    V(lambda e: e.scalar_tensor_tensor(out=osq[g][:], in0=pb[sbk][:, :], scalar=1.0 / 128, in1=ssb[g][1][:],
                                                       op0=ALU.mult, op1=ALU.add), reads=[pbB[sbk], ssbB[g][1]], writes=[osqB[g]])

                def fin_b(g=g, h=h, qt=qt, q0=q0):
                    A(lambda e: e.activation(out=osq[g][:], in_=osq[g][:], func=AF.Ln), reads=[osqB[g]], writes=[osqB[g]])
                    A(lambda e: e.activation(out=osq[g][:], in_=osq[g][:], func=AF.Exp, scale=-0.5), writes=[osqB[g]])
                    V(lambda e: e.scalar_tensor_tensor(out=mixT[:, 4 + h, q0:q0 + 512], in0=ot[g][:], scalar=sm[:, 7:8],
                                                       in1=osq[g][:], op0=ALU.mult, op1=ALU.mult),
                      reads=[smB, osqB[g], otB[g]], writes=[mixB[4 + h][qt]])
                fin_pending.append((fin_a, fin_b))
                if gcnt <= 4:
                    sbk = 4 + (scnt % 4)
                    scnt += 1
                    ga_piece(gcnt - 1, sbk)
        for fa, fb in fin_pending:
            fa()
            fb()
        fin_pending = []

        if debug and seq == 0:
            sy.barrier()
            kd = sy.dma_sem("dbg")
            sy.dma("sp", kd, dbg_mix[:, :], mixT[:].rearrange("p c t -> p (c t)"))
            sy.dma("sp", kd, dbg_q[:, :], qT[:].rearrange("p c t -> p (c t)"))
            sy.dma("sp", kd, dbg_k[:, :], kT[:].rearrange("p c t -> p (c t)"))
            sy.dma("sp", kd, dbg_v[:, :], vt[:].rearrange("p c t -> p (c t)"))
        sy.barrier()
        ar.p = region
        wdn = ar.alloc("wdn", [128, NFF, 1024], BF16)
        g2bc = ar.alloc("g2bc", [128, 1024], F32)
        gfbc = ar.alloc("gfbc", [128, 1024], F32)
        x1 = ar.alloc("x1", [128, 2, 4, 1024], F32)
        h2T = ar.alloc("h2T", [128, 8, 512], BF16)
        actT = ar.alloc("actT", [128, NFF, 512], BF16)
        xr = [ar.alloc(f"xr{i}", [128, 1024], F32) for i in range(2)]
        hb2 = [ar.alloc(f"hb2{i}", [128, 1024], BF16) for i in range(2)]
        wus = [ar.alloc(f"wus{i}", [128, 8, 256], BF16) for i in range(3)]
        accg = [ar.alloc(f"accg{i}", [128, 512], F32) for i in range(2)]
        accu = [ar.alloc(f"accu{i}", [128, 512], F32) for i in range(2)]
        sgt = [ar.alloc(f"sgt{i}", [128, 512], F32) for i in range(2)]
        junk = ar.alloc("junk", [128, 1024], BF16)
        wdnB, g2B, gfB = sy.buf("wdn"), sy.buf("g2"), sy.buf("gf")
        x1B = [[sy.buf(f"x1{p}{i}") for i in range(4)] for p in range(2)]
        h2B = [sy.buf(f"h2{i}") for i in range(4)]
        actB = [sy.buf(f"act{i}") for i in range(NFF)]
        xrB = [sy.buf(f"xr{i}") for i in range(2)]
        hb2B = [sy.buf(f"hb2{i}") for i in range(2)]
        wusB = [sy.buf(f"wus{i}") for i in range(3)]
        accgB = [sy.buf(f"accg{i}") for i in range(2)]
        accuB = [sy.buf(f"accu{i}") for i in range(2)]
        sgB = [sy.buf(f"sg{i}") for i in range(2)]
        junkB = sy.buf("junk")
        k_f0 = sy.dma_sem(f"f0_{seq}")
        k_f1 = sy.dma_sem(f"f1_{seq}")
        k_f2 = sy.dma_sem(f"f2_{seq}")
        k_xr = [sy.dma_sem(f"xr{seq}_{i}") for i in range(2)]
        k_wus = [sy.dma_sem(f"wus{seq}_{i}") for i in range(3)]
        k_out = [sy.dma_sem(f"out{seq}_{i}") for i in range(4)]
        out_keys += k_out

        sy.dma("sp", k_f0, g2bc[:], g3_d[1:2, :].partition_broadcast(128), writes=[g2B])
        sy.dma("sp", k_f1, gfbc[:], g3_d[2:3, :].partition_broadcast(128), writes=[gfB])

        def load_xr(n):
            r0 = seq * S + n * 128
            sy.dma("sp", k_xr[n % 2], xr[n % 2][:], x_d[r0:r0 + 128, :], writes=[xrB[n % 2]])

        wu_n = {"n": 0}

        def load_wup(c):
            s3 = wu_n["n"] % 3
            wu_n["n"] += 1
            sy.dma("sp", k_wus[s3], wus[s3][:],
                   wupb_d[c * 256:(c + 1) * 256, :].rearrange("(p a) n -> p (a n)", a=2).rearrange("p (k c) -> p k c", k=8),
                   reads=[wupbB], writes=[wusB[s3]])
            return s3

        load_xr(0)
        load_xr(1)
        sy.dma("sp", k_f2, wdn[:], wdnb_d.rearrange("(c p) n -> p c n", p=128), reads=[wdnbB], writes=[wdnB])

        def p3_mm(tl, sub):
            n = tl * 4 + sub
            lc = n * 128
            par = tl % 2
            bp = (4, 5)
            for half in range(2):
                for c in range(8):
                    T(lambda e: e.matmul(pb[bp[half]][:, :], lhsT=mixT[:, c, lc:lc + 128],
                                         rhs=wout[:, c, half * 512:(half + 1) * 512], start=(c == 0), stop=(c == 7)),
                      reads=[mixB[c][tl], woutB], writes=[pbB[bp[half]]], inc=(c == 7))
                V(lambda e: e.tensor_tensor(out=x1[:, par, sub, half * 512:(half + 1) * 512], in0=pb[bp[half]][:, :],
                                            in1=xr[n % 2][:, half * 512:(half + 1) * 512], op=ALU.add),
                  reads=[pbB[bp[half]], xrB[n % 2]], writes=[x1B[par][sub]])
            if n + 2 < 16:
                load_xr(n + 2)
            xs = n % 2
            rstd, sb = rms_stat(2 + xs, x1[:, par, sub, :], junk[:], [x1B[par][sub]], junkB, D, 0)
            V(lambda e: e.scalar_tensor_tensor(out=hb2[xs][:], in0=x1[:, par, sub, :], scalar=rstd, in1=g2bc[:],
                                               op0=ALU.mult, op1=ALU.mult),
              reads=[x1B[par][sub], sb, g2B], writes=[hb2B[xs]], strict=[sb])

        def p3_tr(tl, sub):
            n = tl * 4 + sub
            xs = n % 2
            for kc in range(8):
                T(lambda e: e.transpose(out=pbT[:, kc * 128:(kc + 1) * 128], in_=hb2[xs][:, kc * 128:(kc + 1) * 128],
                                        identity=ident[:]),
                  reads=[hb2B[xs], smB], writes=[pbTB], inc=(kc == 7))
            A(lambda e: e.activation(out=h2T[:, :, sub * 128:(sub + 1) * 128],
                                     in_=pbT[:].rearrange("p (k c) -> p k c", k=8), func=AF.Copy),
              reads=[pbTB], writes=[h2B[sub]])

        def p3_pieces(tl):
            return [lambda: p3_mm(tl, 0), lambda: p3_mm(tl, 1), lambda: p3_tr(tl, 0), lambda: p3_mm(tl, 2),
                    lambda: p3_tr(tl, 1), lambda: p3_mm(tl, 3), lambda: p3_tr(tl, 2), lambda: p3_tr(tl, 3)]

        def ffn_up(tl):
            slots = [load_wup(0), load_wup(1)]
            for c in range(NFF):
                if c + 2 < NFF:
                    slots.append(load_wup(c + 2))
                s3 = slots[c]
                a2 = c % 2
                banks = (next_bank(6), next_bank(6))
                for gu in range(2):
                    bi = banks[gu]
                    for kc in range(8):
                        T(lambda e: e.matmul(pb[bi][:, :], lhsT=wus[s3][:, kc, gu * 128:(gu + 1) * 128], rhs=h2T[:, kc, :],
                                             start=(kc == 0), stop=(kc == 7)),
                          reads=[wusB[s3]] + h2B, writes=[pbB[bi]], inc=(kc == 7))
                for gu in range(2):
                    bi = banks[gu]
                    cc = c + gu * NFF
                    acc, accB_ = (accg[a2], accgB[a2]) if gu == 0 else (accu[a2], accuB[a2])
                    A(lambda e: e.activation(out=acc[:], in_=pb[bi][:, :], func=AF.Identity, scale=CW(2, cc), bias=CB(cc)),
                      reads=[pbB[bi], constB], writes=[accB_])
                    V(lambda e: e.scalar_tensor_tensor(out=acc[:, 1:512], in0=pb[bi][:, 0:511], scalar=CW(1, cc),
                                                       in1=acc[:, 1:512], op0=ALU.mult, op1=ALU.add),
                      reads=[pbB[bi]], writes=[accB_])
                    V(lambda e: e.scalar_tensor_tensor(out=acc[:, 2:512], in0=pb[bi][:, 0:510], scalar=CW(0, cc),
                                                       in1=acc[:, 2:512], op0=ALU.mult, op1=ALU.add),
                      reads=[pbB[bi]], writes=[accB_])
                    if tl > 0:
                        hp = (tl - 1) % 2
                        V(lambda e: e.scalar_tensor_tensor(out=acc[:, 0:1], in0=hal[:, hp, cc, 1:2], scalar=CW(1, cc),
                                                           in1=acc[:, 0:1], op0=ALU.mult, op1=ALU.add),
                          reads=[halB[hp][cc]], writes=[accB_])
                        V(lambda e: e.scalar_tensor_tensor(out=acc[:, 0:2], in0=hal[:, hp, cc, 0:2], scalar=CW(0, cc),
                                                           in1=acc[:, 0:2], op0=ALU.mult, op1=ALU.add),
                          reads=[halB[hp][cc]], writes=[accB_])
                    if tl < 3:
                        A(lambda e: e.activation(out=hal[:, tl % 2, cc, :], in_=pb[bi][:, 510:512], func=AF.Copy),
                          reads=[pbB[bi]], writes=[halB[tl % 2][cc]])
                A(lambda e: e.activation(out=sgt[a2][:], in_=accg[a2][:], func=AF.Silu), reads=[accgB[a2]], writes=[sgB[a2]])
                P(lambda e: e.tensor_tensor(out=actT[:, c, :], in0=sgt[a2][:], in1=accu[a2][:], op=ALU.mult),
                  reads=[sgB[a2], accuB[a2]], writes=[actB[c]])

        def ffn_down_half(tl, sub, half):
            n = tl * 4 + sub
            r0 = seq * S + n * 128
            par = tl % 2
            bp = (0, 1) if sub % 2 == 0 else (2, 3)
            for c in range(NFF):
                T(lambda e: e.matmul(pb[bp[half]][:, :], lhsT=actT[:, c, sub * 128:(sub + 1) * 128],
                                     rhs=wdn[:, c, half * 512:(half + 1) * 512], start=(c == 0), stop=(c == NFF - 1)),
                  reads=[actB[c], wdnB], writes=[pbB[bp[half]]], inc=(c == NFF - 1))
            V(lambda e: e.tensor_tensor(out=x1[:, par, sub, half * 512:(half + 1) * 512], in0=pb[bp[half]][:, :],
                                        in1=x1[:, par, sub, half * 512:(half + 1) * 512], op=ALU.add),
              reads=[pbB[bp[half]]], writes=[x1B[par][sub]])
            if half == 1:
                rstd, sb = rms_stat(4 + (n % 2), x1[:, par, sub, :], junk[:], [x1B[par][sub]], junkB, D, 0)
                V(lambda e: e.scalar_tensor_tensor(out=x1[:, par, sub, :], in0=x1[:, par, sub, :], scalar=rstd, in1=gfbc[:],
                                                   op0=ALU.mult, op1=ALU.mult),
                  reads=[sb, gfB], writes=[x1B[par][sub]], strict=[sb])
                sy.dma("pool", k_out[sub], out_d[r0:r0 + 128, :], x1[:, par, sub, :], reads=[x1B[par][sub]])

        for f in p3_pieces(0):
            f()
        for tl in range(4):
            ffn_up(tl)
            pieces = p3_pieces(tl + 1) if tl < 3 else []
            pi = 0
            for sub in range(4):
                for half in range(2):
                    ffn_down_half(tl, sub, half)
                    if pi < len(pieces):
                        pieces[pi]()
                        pi += 1

    sy.final_wait("pool", out_keys)
    sy.barrier()
    es.close()
    return nc


_NC = None


def _host_consts():
    inv = (10000.0 ** (-(np.arange(0, 64, 2, dtype=np.float32) / np.float32(64)))).astype(np.float32)
    ang = (np.arange(S, dtype=np.float32)[:, None] * inv[None, :]).astype(np.float32)
    cos = np.cos(ang).astype(np.float32).T
    sin = np.sin(ang).astype(np.float32).T
    cs = np.zeros((2, 128, S), np.float32)
    for p in range(128):
        d = p % 64
        i = d % 32
        cs[0, p] = cos[i]
        cs[1, p] = -sin[i] if d < 32 else sin[i]
    cm = np.zeros((4, 128, 128), np.float32)
    cm[0] = np.eye(128, dtype=np.float32)
    for p in range(128):
        d = p % 64
        partner = p + 32 if d < 32 else p - 32
        cm[1, partner, p] = 1.0
    kk = np.arange(128)[:, None]
    qq = np.arange(128)[None, :]
    cm[2] = (kk <= qq).astype(np.float32)
    cm[3] = 1.0
    return cs.reshape(256, S), cm.reshape(512, 128)


def kernel(x, norm1_g, w_in, short_conv_w, mix_a_norm_g, lambda_q1, lambda_k1, lambda_q2, lambda_k2,
           subln_g, w_out, norm2_g, w_up, ffn_conv_w, ffn_conv_b, w_down, final_g):
    global _NC
    f = lambda a: np.ascontiguousarray(np.asarray(a, dtype=np.float32))
    x = f(x)
    w_in0 = f(w_in)[0]
    order = []
    for i in range(4):
        order += [512 + 128 * i, 1024 + 128 * i, 128 * i]
    order += [1536 + 128 * h for h in range(4)] + [2048 + 128 * h for h in range(4)]
    w_in_fm = np.stack([w_in0[:, c:c + 128].reshape(8, 128, 128).transpose(1, 0, 2) for c in order])
    w_in_fm = np.ascontiguousarray(w_in_fm).reshape(20 * 128, 1024)
    w_v = np.ascontiguousarray(w_in0[:, 2560:3072])
    w_up0 = f(w_up)[0]
    chunks = []
    for c in range(NFF):
        gcol = w_up0[:, c * 128:(c + 1) * 128].reshape(8, 128, 128)
        ucol = w_up0[:, DFF + c * 128:DFF + (c + 1) * 128].reshape(8, 128, 128)
        chunks.append(np.concatenate([gcol, ucol], axis=2).transpose(1, 0, 2))
    w_up_fm = np.ascontiguousarray(np.stack(chunks)).reshape(NFF * 128 * 2, 1024)
    g3 = np.ascontiguousarray(np.stack([f(norm1_g)[0], f(norm2_g)[0], f(final_g)]))
    lamv = np.concatenate([f(lambda_q1)[0], f(lambda_k1)[0], f(lambda_q2)[0], f(lambda_k2)[0]])[None, :]
    cols = np.zeros((128, NCOLS), np.float32)
    fw = f(ffn_conv_w)[0]
    for j in range(3):
        cols[:, j * 44:(j + 1) * 44] = fw[j].reshape(44, 128).T
    cols[:, 132:176] = f(ffn_conv_b)[0].reshape(44, 128).T
    sw = f(short_conv_w)[0]
    for j in range(3):
        cols[:, 176 + 4 * j:180 + 4 * j] = sw[j].reshape(4, 128).T
    cols[:, 188:192] = f(mix_a_norm_g)[0].reshape(4, 128).T
    cols[:, 192] = f(subln_g)[0]
    cs, cm = _host_consts()
    shared = {
        "w_in_fm": w_in_fm, "w_v": w_v, "w_out": f(w_out)[0], "w_up_fm": w_up_fm, "w_down": f(w_down)[0],
        "g3": g3, "lamv": np.ascontiguousarray(lamv), "cols": cols, "cs": cs, "cmat": cm,
    }
    xs = x.reshape(NCORES, TOK, D)
    in_maps = [dict(shared, x=np.ascontiguousarray(xs[c])) for c in range(NCORES)]
    if _NC is None:
        _NC = build_nc()
    res = run_bass_kernel_spmd(_NC, in_maps, core_ids=list(range(NCORES)))
    out = np.stack([np.asarray(r["out"]) for r in res.results]).reshape(16, S, D)
    return out.astype(np.float32)
```

```python
import numpy as np
import concourse.bass as bass
import concourse.mybir as mybir
from concourse.bass_utils import run_bass_kernel_spmd

F32 = mybir.dt.float32
BF16 = mybir.dt.bfloat16
AF = mybir.ActivationFunctionType
ALU = mybir.AluOpType
AX = mybir.AxisListType

NCORES = 8
D = 1024
S = 2048
NSEQ = 2
TOK = NSEQ * S
DFF = 2816
NFF = DFF // 128
SBUF_BASE = 16512
SBUF_LIMIT = 229344
NCOLS = 193


class Buf:
    __slots__ = ("name", "w", "r")

    def __init__(self, name):
        self.name = name
        self.w = None
        self.r = {}


class Sync:
    def __init__(self, nc, es):
        self.nc = nc
        self.es = es
        self.engs = {"pe": nc.tensor, "act": nc.scalar, "dve": nc.vector, "pool": nc.gpsimd, "sp": nc.sync}
        self.semh = {}
        self.cnt = {}
        for e in self.engs:
            self.semh[e] = es.enter_context(nc.semaphore("sem_" + e))
            self.cnt[e] = 0
        self.seen = {e: {} for e in self.engs}
        self.bufs = []
        self.dma_keys = []
        self.shared_keys = set()

    def buf(self, name):
        b = Buf(name)
        self.bufs.append(b)
        return b

    def dma_sem(self, name):
        key = "d_" + name
        self.semh[key] = self.es.enter_context(self.nc.semaphore("sem_" + key))
        self.cnt[key] = 0
        self.dma_keys.append(key)
        return key

    def _waits(self, e, reads, writes, same_ok, strict=()):
        waits = {}
        for b in strict:
            if b.w is not None and waits.get(b.w[0], 0) < b.w[1]:
                waits[b.w[0]] = b.w[1]

        def need(ev):
            if ev is None:
                return
            k, v = ev
            if same_ok and k == e:
                return
            if waits.get(k, 0) < v:
                waits[k] = v

        for b in reads:
            need(b.w)
        for b in writes:
            need(b.w)
            for k, v in b.r.items():
                need((k, v))
        eng = self.engs[e]
        for k, v in waits.items():
            if self.seen[e].get(k, 0) >= v:
                continue
            eng.wait_ge(self.semh[k], v)
            self.seen[e][k] = v

    def op(self, e, fn, reads=(), writes=(), inc=True, strict=()):
        self._waits(e, reads, writes, same_ok=True, strict=strict)
        ins = fn(self.engs[e])
        if inc:
            self.cnt[e] += 1
            ins.then_inc(self.semh[e], 1)
            v = self.cnt[e]
        else:
            v = self.cnt[e] + 1
        for b in reads:
            if b.r.get(e, 0) < v:
                b.r[e] = v
        for b in writes:
            b.w = (e, v)
            b.r = {}

    def dma(self, q, key, out, in_, reads=(), writes=(), after=()):
        self._waits(q, list(reads) + list(after), writes, same_ok=False)
        if self.cnt[key] and self.seen[q].get(key, 0) < self.cnt[key] and key not in self.shared_keys:
            self.engs[q].wait_ge(self.semh[key], self.cnt[key])
            self.seen[q][key] = self.cnt[key]
        ins = self.engs[q].dma_start(out=out, in_=in_)
        self.cnt[key] += 16
        ins.then_inc(self.semh[key], 16)
        v = self.cnt[key]
        for b in reads:
            if b.r.get(key, 0) < v:
                b.r[key] = v
        for b in writes:
            b.w = (key, v)
            b.r = {}

    def barrier(self):
        keys = list(self.engs.keys()) + self.dma_keys
        for e in self.engs:
            for k in keys:
                if k == e:
                    continue
                v = self.cnt[k]
                if v == 0 or self.seen[e].get(k, 0) >= v:
                    continue
                self.engs[e].wait_ge(self.semh[k], v)
                self.seen[e][k] = v
        for b in self.bufs:
            b.w = None
            b.r = {}

    def final_wait(self, e, keys):
        for k in keys:
            v = self.cnt[k]
            if v:
                self.engs[e].wait_ge(self.semh[k], v)


class Arena:
    def __init__(self, nc):
        self.nc = nc
        self.p = SBUF_BASE
        self.n = 0

    def alloc(self, name, shape, dtype):
        esz = 2 if dtype == BF16 else 4
        per = esz
        for d in shape[1:]:
            per *= d
        off = (self.p + 31) // 32 * 32
        assert off + per <= SBUF_LIMIT, (name, off, per)
        self.n += 1
        t = self.nc.alloc_sbuf_tensor_at(f"{name}_{self.n}", list(shape), dtype, offset=off)
        self.p = off + per
        return t


def build_nc(debug=False):
    from contextlib import ExitStack

    nc = bass.Bass("TRN2", target_bir_lowering=False)
    dt = nc.dram_tensor
    x_d = dt("x", [TOK, D], F32, kind="ExternalInput").ap()
    win_d = dt("w_in_fm", [20 * 128, 1024], F32, kind="ExternalInput").ap()
    wv_d = dt("w_v", [1024, 512], F32, kind="ExternalInput").ap()
    wout_d = dt("w_out", [1024, 1024], F32, kind="ExternalInput").ap()
    wup_d = dt("w_up_fm", [NFF * 128 * 2, 1024], F32, kind="ExternalInput").ap()
    wdn_d = dt("w_down", [DFF, 1024], F32, kind="ExternalInput").ap()
    g3_d = dt("g3", [3, 1024], F32, kind="ExternalInput").ap()
    lam_d = dt("lamv", [1, 256], F32, kind="ExternalInput").ap()
    cols_d = dt("cols", [128, NCOLS], F32, kind="ExternalInput").ap()
    cs_d = dt("cs", [2 * 128, S], F32, kind="ExternalInput").ap()
    cm_d = dt("cmat", [4 * 128, 128], F32, kind="ExternalInput").ap()
    out_d = dt("out", [TOK, D], F32, kind="ExternalOutput").ap()
    winb_d = dt("w_in_b", [20 * 128, 1024], BF16, kind="Internal").ap()
    wvb_d = dt("w_v_b", [1024, 512], BF16, kind="Internal").ap()
    woutb_d = dt("w_out_b", [1024, 1024], BF16, kind="Internal").ap()
    wupb_d = dt("w_up_b", [NFF * 128 * 2, 1024], BF16, kind="Internal").ap()
    wdnb_d = dt("w_down_b", [DFF, 1024], BF16, kind="Internal").ap()

    if debug:
        dbg_mix = dt("dbg_mix", [128, 8 * S], BF16, kind="ExternalOutput").ap()
        dbg_q = dt("dbg_q", [128, 4 * S], BF16, kind="ExternalOutput").ap()
        dbg_k = dt("dbg_k", [128, 4 * S], BF16, kind="ExternalOutput").ap()
        dbg_v = dt("dbg_v", [128, 16 * 512], BF16, kind="ExternalOutput").ap()
    es = ExitStack()
    sy = Sync(nc, es)
    ar = Arena(nc)

    pb = [nc.alloc_psum_tensor(f"pb{i}", [128, 512], F32) for i in range(8)]
    pbT = pb[7][:, :].bitcast(BF16)
    pbB = [sy.buf(f"pb{i}") for i in range(8)]
    pbTB = pbB[7]

    mixT = ar.alloc("mixT", [128, 8, S], BF16)
    wout = ar.alloc("wout", [128, 8, 1024], BF16)
    cols = ar.alloc("cols", [128, NCOLS], F32)
    cmf = ar.alloc("cmf", [128, 4, 128], F32)
    ident = ar.alloc("ident", [128, 128], BF16)
    pswap = ar.alloc("pswap", [128, 128], BF16)
    tri = ar.alloc("tri", [128, 128], BF16)
    ones_b = ar.alloc("ones_b", [128, 128], BF16)
    lamt = ar.alloc("lamt", [128, 256], F32)
    sm = ar.alloc("sm", [128, 16], F32)
    hal = ar.alloc("hal", [128, 2, 2 * NFF, 2], F32)
    stat = ar.alloc("stat", [128, 8, 4], F32)
    region = ar.p

    mixB = [[sy.buf(f"mix{c}_{t}") for t in range(4)] for c in range(8)]
    woutB = sy.buf("wout")
    constB = sy.buf("const")
    smB = sy.buf("sm")
    halB = [[sy.buf(f"hal{p}_{c}") for c in range(2 * NFF)] for p in range(2)]
    statB = [sy.buf(f"stat{i}") for i in range(8)]

    winbB, wvbB, woutbB, wupbB, wdnbB = (sy.buf(n) for n in ("winb", "wvb", "woutb", "wupb", "wdnb"))

    k_c1 = sy.dma_sem("cast_win")
    k_c2 = sy.dma_sem("cast_wv")
    k_c3 = sy.dma_sem("cast_wout")
    k_c4 = sy.dma_sem("cast_wup")
    k_c5 = sy.dma_sem("cast_wdn")
    sy.dma("pool", k_c2, wvb_d[:, :], wv_d[:, :], writes=[wvbB])

    k_const = sy.dma_sem("const")
    sy.shared_keys.add(k_const)
    sy.dma("sp", k_const, cols[:], cols_d[:, :])
    sy.dma("sp", k_const, cmf[:], cm_d.rearrange("(m p) c -> p m c", p=128))
    sy.dma("sp", k_const, lamt[:], lam_d[0:1, :].partition_broadcast(128))
    constB.w = (k_const, sy.cnt[k_const])
    k_wout = sy.dma_sem("wout")

    V = lambda f, **kw: sy.op("dve", f, **kw)
    A = lambda f, **kw: sy.op("act", f, **kw)
    P = lambda f, **kw: sy.op("pool", f, **kw)
    T = lambda f, **kw: sy.op("pe", f, **kw)

    V(lambda e: e.tensor_copy(out=ident[:], in_=cmf[:, 0, :]), reads=[constB], writes=[smB])
    V(lambda e: e.tensor_copy(out=pswap[:], in_=cmf[:, 1, :]), reads=[constB], writes=[smB])
    V(lambda e: e.tensor_copy(out=tri[:], in_=cmf[:, 2, :]), reads=[constB], writes=[smB])
    V(lambda e: e.tensor_copy(out=ones_b[:], in_=cmf[:, 3, :]), reads=[constB], writes=[smB])
    V(lambda e: e.memset(sm[:, 0:1], 1e-6), writes=[smB])
    V(lambda e: e.memset(sm[:, 1:2], 1e-5), writes=[smB])
    V(lambda e: e.memset(hal[:], 0.0), writes=halB[0] + halB[1])
    lt = ar.alloc("lamtmp", [128, 128], F32)
    V(lambda e: e.tensor_tensor(out=lt[:, 0:64], in0=lamt[:, 0:64], in1=lamt[:, 64:128], op=ALU.mult), reads=[constB], writes=[smB])
    V(lambda e: e.tensor_tensor(out=lt[:, 64:128], in0=lamt[:, 128:192], in1=lamt[:, 192:256], op=ALU.mult), writes=[smB])
    V(lambda e: e.reduce_sum(out=sm[:, 2:3], in_=lt[:, 0:64], axis=AX.X), writes=[smB], strict=[smB])
    V(lambda e: e.reduce_sum(out=sm[:, 3:4], in_=lt[:, 64:128], axis=AX.X), writes=[smB], strict=[smB])
    A(lambda e: e.activation(out=sm[:, 4:6], in_=sm[:, 2:4], func=AF.Exp), reads=[smB], writes=[smB])
    V(lambda e: e.tensor_sub(out=sm[:, 6:7], in0=sm[:, 5:6], in1=sm[:, 4:5]), reads=[smB], writes=[smB])
    V(lambda e: e.tensor_scalar_add(out=sm[:, 6:7], in0=sm[:, 6:7], scalar1=-0.2), writes=[smB], strict=[smB])
    V(lambda e: e.tensor_scalar_mul(out=sm[:, 7:8], in0=cols[:, 192:193], scalar1=0.8), reads=[constB], writes=[smB])
    region = ar.p

    CW = lambda j, c: cols[:, j * 44 + c:j * 44 + c + 1]
    CB = lambda c: cols[:, 132 + c:133 + c]
    SCW = lambda j, i: cols[:, 176 + j * 4 + i:177 + j * 4 + i]
    GA = lambda i: cols[:, 188 + i:189 + i]

    rot = {"i": 0}

    def next_bank(n=6):
        i = rot["i"] % n
        rot["i"] += 1
        return i

    def rms_stat(slot, src_ap, junk_ap, srcB, junkB, n_feat, eps_col):
        sb = statB[slot]
        A(lambda e: e.activation(out=junk_ap, in_=src_ap, func=AF.Square, accum_out=stat[:, slot, 0:1]),
          reads=srcB, writes=[junkB, sb])
        A(lambda e: e.activation(out=stat[:, slot, 1:2], in_=stat[:, slot, 0:1], func=AF.Sqrt,
                                 scale=1.0 / n_feat, bias=sm[:, eps_col:eps_col + 1]), reads=[smB], writes=[sb], strict=[sb])
        V(lambda e: e.reciprocal(out=stat[:, slot, 2:3], in_=stat[:, slot, 1:2]), reads=[sb], writes=[sb])
        return stat[:, slot, 2:3], sb

    out_keys = []

    for seq in range(NSEQ):
        if seq > 0:
            sy.barrier()
        ar.p = region
        qT = ar.alloc("qT", [128, 4, S], BF16)
        kT = ar.alloc("kT", [128, 4, S], BF16)
        vt = ar.alloc("vt", [128, 16, 512], BF16)
        hT = ar.alloc("hT", [128, 8, S], BF16)
        p2_base = ar.p - 32768
        p01_base = ar.p
        wv = ar.alloc("wv", [128, 8, 512], BF16)
        g1bc = ar.alloc("g1bc", [128, 1024], F32)
        xt = [ar.alloc(f"xt{i}", [128, 1024], F32) for i in range(4)]
        hb = [ar.alloc(f"hb{i}", [128, 1024], BF16) for i in range(2)]
        ar_save = ar.p
        ar.p = p01_base + 65536
        wsl = [ar.alloc(f"wsl{i}", [128, 8, 128], BF16) for i in range(3)]
        wsl_end = ar.p
        ar.p = ar_save
        wslB = [sy.buf(f"wsl{i}") for i in range(3)]
        k_wsl = [sy.dma_sem(f"wsl{seq}_{i}") for i in range(3)]

        def load_win(ci):
            s3 = ci % 3
            sy.dma("sp", k_wsl[s3], wsl[s3][:],
                   winb_d[ci * 128:(ci + 1) * 128, :].rearrange("p (k c) -> p k c", k=8),
                   reads=[winbB], writes=[wslB[s3]])
        qB = [[sy.buf(f"q{h}_{t}") for t in range(4)] for h in range(4)]
        kB = [[sy.buf(f"k{h}_{t}") for t in range(4)] for h in range(4)]
        vB = [sy.buf(f"v{t}") for t in range(16)]
        hTB = [sy.buf(f"hT{t}") for t in range(16)]
        wvB, g1B = sy.buf("wv"), sy.buf("g1")
        xtB = [sy.buf(f"xt{i}") for i in range(4)]
        hbB = [sy.buf(f"hb{i}") for i in range(2)]
        k_xt = [sy.dma_sem(f"xt{seq}_{i}") for i in range(4)]
        k_m0 = sy.dma_sem(f"m0_{seq}")
        k_m1 = sy.dma_sem(f"m1_{seq}")

        sy.dma("sp", k_m0, g1bc[:], g3_d[0:1, :].partition_broadcast(128), writes=[g1B])

        def load_x(i):
            r0 = seq * S + i * 128
            sy.dma("sp", k_xt[i % 4], xt[i % 4][:], x_d[r0:r0 + 128, :], writes=[xtB[i % 4]])

        junk0 = ar.alloc("junk0", [128, 1024], BF16)
        junk0B = sy.buf("junk0")
        for i0 in range(4):
            load_x(i0)
        sy.dma("sp", k_m1, wv[:], wvb_d.rearrange("(c p) n -> p c n", p=128), reads=[wvbB], writes=[wvB])
        if seq == 0:
            sy.dma("pool", k_c1, winb_d[:, :], win_d[:, :], after=[xtB[0], xtB[1], xtB[2], xtB[3]], writes=[winbB])
        st = {0: rms_stat(0, xt[0][:], junk0[:], [xtB[0]], junk0B, D, 0)}
        for i in range(16):
            sl = i % 2
            x4 = i % 4
            rstd, sb = st[i]
            V(lambda e: e.scalar_tensor_tensor(out=hb[sl][:], in0=xt[x4][:], scalar=rstd, in1=g1bc[:],
                                               op0=ALU.mult, op1=ALU.mult),
              reads=[xtB[x4], sb, g1B], writes=[hbB[sl]], strict=[sb])
            if i + 4 < 16:
                load_x(i + 4)
            if i + 1 < 16:
                st[i + 1] = rms_stat((i + 1) % 2, xt[(i + 1) % 4][:], junk0[:], [xtB[(i + 1) % 4]], junk0B, D, 0)
            for kc in range(8):
                T(lambda e: e.transpose(out=pbT[:, kc * 128:(kc + 1) * 128], in_=hb[sl][:, kc * 128:(kc + 1) * 128],
                                        identity=ident[:]),
                  reads=[hbB[sl], smB], writes=[pbTB], inc=(kc == 7))
            A(lambda e: e.activation(out=hT[:, :, i * 128:(i + 1) * 128],
                                     in_=pbT[:].rearrange("p (k c) -> p k c", k=8), func=AF.Copy),
              reads=[pbTB], writes=[hTB[i]])

            def vproj(i=i):
                bi = next_bank(2)
                for kc in range(8):
                    T(lambda e: e.matmul(pb[bi][:, :], lhsT=hT[:, kc, i * 128:(i + 1) * 128], rhs=wv[:, kc, :],
                                         start=(kc == 0), stop=(kc == 7)),
                      reads=[hTB[i], wvB], writes=[pbB[bi]], inc=(kc == 7))
                A(lambda e: e.activation(out=vt[:, i, :], in_=pb[bi][:, :], func=AF.Copy), reads=[pbB[bi]], writes=[vB[i]])
            if i > 0:
                vproj(i - 1)
            if i == 15:
                vproj(15)

        load_win(0)
        load_win(1)
        sy.barrier()
        ar.p = p01_base
        cs = ar.alloc("cs", [128, 2, S], F32)
        gcs = [ar.alloc(f"gcs{i}", [128, 512], F32) for i in range(4)]
        cx = [ar.alloc(f"cx{i}", [128, 516], F32) for i in range(2)]
        yv = [ar.alloc(f"yv{i}", [128, 512], F32) for i in range(4)]
        yaf = [ar.alloc(f"yaf{i}", [128, 512], F32) for i in range(2)]
        sqt = [ar.alloc(f"sqt{i}", [128, 512], F32) for i in range(1)] * 2
        accss = ar.alloc("accss", [128, S], F32)
        stdt = [ar.alloc(f"stdt{i}", [128, 512], F32) for i in range(1)] * 2
        qraw = [ar.alloc(f"qraw{i}", [128, 512], BF16) for i in range(2)]
        rt = [ar.alloc(f"rt{i}", [128, 512], F32) for i in range(2)]
        ru = [ar.alloc(f"ru{i}", [128, 512], F32) for i in range(2)]
        assert ar.p <= p01_base + 65536
        p1_end = wsl_end
        csB = sy.buf("cs")
        gcsB = [sy.buf(f"gcs{i}") for i in range(4)]
        cxB = [sy.buf(f"cx{i}") for i in range(2)]
        yvB = [sy.buf(f"yv{i}") for i in range(4)]
        yafB = [sy.buf(f"yaf{i}") for i in range(2)]
        sqtB = [sy.buf(f"sqt{i}") for i in range(1)] * 2
        accB = [sy.buf(f"acc{i}") for i in range(4)]
        stdB = [sy.buf(f"std{i}") for i in range(1)] * 2
        qrawB = [sy.buf(f"qraw{i}") for i in range(2)]
        rtB = [sy.buf(f"rt{i}") for i in range(2)]
        ruB = [sy.buf(f"ru{i}") for i in range(2)]
        k_cs = sy.dma_sem(f"cs{seq}")

        if seq == 0:
            sy.dma("pool", k_c3, woutb_d[:, :], wout_d[:, :], writes=[woutbB])
        sy.dma("sp", k_cs, cs[:], cs_d.rearrange("(m p) t -> p m t", p=128), writes=[csB])
        V(lambda e: e.memset(cx[0][:, 0:2], 0.0), writes=[cxB[0]])
        V(lambda e: e.memset(cx[1][:, 0:2], 0.0), writes=[cxB[1]])

        deferred = []
        cnt = {"gc": 0, "cx": 0, "ya": 0, "qr": 0}
        for ci in range(20):
            if ci + 2 < 20:
                load_win(ci + 2)
            s3 = ci % 3
            for tt in range(4):
                bi = next_bank(6)
                c0 = tt * 512
                for kc in range(8):
                    T(lambda e: e.matmul(pb[bi][:, :], lhsT=wsl[s3][:, kc, :], rhs=hT[:, kc, c0:c0 + 512],
                                         start=(kc == 0), stop=(kc == 7)),
                      reads=[wslB[s3]], writes=[pbB[bi]], inc=(kc == 7))
                for f in deferred:
                    f()
                deferred = []
                if ci < 12:
                    i, typ = ci // 3, ci % 3
                    if typ == 0:
                        g = tt
                        A(lambda e: e.activation(out=gcs[g][:], in_=pb[bi][:, :], func=AF.Copy),
                          reads=[pbB[bi]], writes=[gcsB[g]])
                    elif typ == 1:
                        g = tt
                        c = cnt["cx"] % 2
                        cnt["cx"] += 1
                        if tt == 0:
                            P(lambda e: e.memset(cx[c][:, 0:2], 0.0), writes=[cxB[c]])
                        V(lambda e: e.tensor_tensor(out=cx[c][:, 2:514], in0=pb[bi][:, :], in1=gcs[g][:], op=ALU.mult),
                          reads=[pbB[bi], gcsB[g]], writes=[cxB[c]])
                        if tt < 3:
                            P(lambda e: e.tensor_copy(out=cx[1 - c][:, 0:2], in_=cx[c][:, 512:514]),
                              reads=[cxB[c]], writes=[cxB[1 - c]])
                        A(lambda e: e.activation(out=yv[tt][:], in_=cx[c][:, 2:514], func=AF.Copy, scale=SCW(2, i)),
                          reads=[cxB[c], constB], writes=[yvB[tt]])
                        V(lambda e: e.scalar_tensor_tensor(out=yv[tt][:], in0=cx[c][:, 1:513], scalar=SCW(1, i),
                                                           in1=yv[tt][:], op0=ALU.mult, op1=ALU.add),
                          reads=[cxB[c]], writes=[yvB[tt]])
                        V(lambda e: e.scalar_tensor_tensor(out=yv[tt][:], in0=cx[c][:, 0:512], scalar=SCW(0, i),
                                                           in1=yv[tt][:], op0=ALU.mult, op1=ALU.add),
                          reads=[cxB[c]], writes=[yvB[tt]])
                    else:
                        a = cnt["ya"] % 2
                        cnt["ya"] += 1
                        V(lambda e: e.tensor_tensor(out=yaf[a][:], in0=pb[bi][:, :], in1=yv[tt][:], op=ALU.mult),
                          reads=[pbB[bi], yvB[tt]], writes=[yafB[a]])
                        A(lambda e: e.activation(out=mixT[:, i, c0:c0 + 512], in_=yaf[a][:], func=AF.Copy),
                          reads=[yafB[a]], writes=[mixB[i][tt]])
                        if i == 0:
                            A(lambda e: e.activation(out=accss[:, c0:c0 + 512], in_=yaf[a][:], func=AF.Square),
                              reads=[yafB[a]], writes=[accB[tt]])
                        else:
                            A(lambda e: e.activation(out=sqt[a][:], in_=yaf[a][:], func=AF.Square),
                              reads=[yafB[a]], writes=[sqtB[a]])
                            P(lambda e: e.tensor_tensor(out=accss[:, c0:c0 + 512], in0=accss[:, c0:c0 + 512],
                                                        in1=sqt[a][:], op=ALU.add),
                              reads=[sqtB[a]], writes=[accB[tt]])
                else:
                    h = (ci - 12) % 4
                    dstT, dstB = (qT, qB) if ci < 16 else (kT, kB)
                    r = cnt["qr"] % 2
                    cnt["qr"] += 1
                    A(lambda e: e.activation(out=qraw[r][:], in_=pb[bi][:, :], func=AF.Copy),
                      reads=[pbB[bi]], writes=[qrawB[r]])
                    V(lambda e: e.tensor_tensor(out=rt[r][:], in0=pb[bi][:, :], in1=cs[:, 0, c0:c0 + 512], op=ALU.mult),
                      reads=[pbB[bi], csB], writes=[rtB[r]])

                    def rope_tail(r=r, h=h, tt=tt, c0=c0, dstT=dstT, dstB=dstB):
                        b2 = next_bank(6)
                        T(lambda e: e.matmul(pb[b2][:, :], lhsT=pswap[:], rhs=qraw[r][:], start=True, stop=True),
                          reads=[qrawB[r], smB], writes=[pbB[b2]])
                        V(lambda e: e.tensor_tensor(out=ru[r][:], in0=pb[b2][:, :], in1=cs[:, 1, c0:c0 + 512], op=ALU.mult),
                          reads=[pbB[b2], csB], writes=[ruB[r]])
                        P(lambda e: e.tensor_tensor(out=dstT[:, h, c0:c0 + 512], in0=rt[r][:], in1=ru[r][:], op=ALU.add),
                          reads=[rtB[r], ruB[r]], writes=[dstB[h][tt]])

                    deferred.append(rope_tail)
            if ci == 11:
                def ga_piece(tt, bank):
                    c0 = tt * 512
                    T(lambda e: e.matmul(pb[bank][:, :], lhsT=cmf[:, 3, :], rhs=accss[:, c0:c0 + 512], start=True, stop=True),
                      reads=[constB], writes=[pbB[bank]])
                    A(lambda e: e.activation(out=stdt[0][:], in_=pb[bank][:, :], func=AF.Ln, scale=1.0 / 512,
                                             bias=sm[:, 0:1]), reads=[pbB[bank], smB], writes=[stdB[0]])
                    A(lambda e: e.activation(out=stdt[0][:], in_=stdt[0][:], func=AF.Exp, scale=-0.5), writes=[stdB[0]])
                    for i in range(4):
                        V(lambda e: e.scalar_tensor_tensor(out=mixT[:, i, c0:c0 + 512], in0=mixT[:, i, c0:c0 + 512],
                                                           scalar=GA(i), in1=stdt[0][:], op0=ALU.mult, op1=ALU.mult),
                          reads=[stdB[0], constB], writes=[mixB[i][tt]])
        for f in deferred:
            f()
        deferred = []

        sy.barrier()
        ar.p = p2_base
        eT = [ar.alloc(f"eT{i}", [128, 512], BF16) for i in range(6)]
        ssb = [[ar.alloc(f"ssb{g}{j}", [128, 512], F32) for j in range(2)] for g in range(2)]
        osb = [[ar.alloc(f"osb{g}{j}", [128, 512], F32) for j in range(2)] for g in range(2)]
        ot = [ar.alloc(f"ot{i}", [128, 512], F32) for i in range(2)]
        osq = [ar.alloc(f"osq{i}", [128, 512], F32) for i in range(2)]
        assert ar.p <= p01_base
        eTB = [sy.buf(f"eT{i}") for i in range(6)]
        ssbB = [[sy.buf(f"ssb{g}{j}") for j in range(2)] for g in range(2)]
        osbB = [[sy.buf(f"osb{g}{j}") for j in range(2)] for g in range(2)]
        otB = [sy.buf(f"ot{i}") for i in range(2)]
        osqB = [sy.buf(f"osq{i}") for i in range(2)]
        if seq == 0:
            sy.dma("pool", k_c4, wupb_d[:, :], wup_d[:, :], writes=[wupbB])
            sy.dma("pool", k_c5, wdnb_d[:, :], wdn_d[:, :], writes=[wdnbB])
            sy.dma("sp", k_wout, wout[:], woutb_d.rearrange("(c p) n -> p c n", p=128), reads=[woutbB], writes=[woutB])
        ecnt = 0
        scnt = 0
        gcnt = 0
        O = [0, 1]
        Sm = [2, 3]
        fin_pending = []
        for h in range(4):
            for qt in range(4):
                nk = 4 * (qt + 1)
                q0 = qt * 512
                pend = []
                for kt in range(nk):
                    di = kt - 4 * qt
                    c0 = 128 * di if di > 0 else 0
                    cur = []
                    for j in range(2):
                        sbk = 4 + (scnt % 4)
                        scnt += 1
                        es_ = ecnt % 6
                        ecnt += 1
                        pr = slice(64 * j, 64 * j + 64)
                        T(lambda e: e.matmul(pb[sbk][:, c0:512], lhsT=kT[pr, h, kt * 128:(kt + 1) * 128],
                                             rhs=qT[pr, h, q0 + c0:q0 + 512], start=True, stop=True),
                          reads=[kB[h][kt // 4], qB[h][qt]], writes=[pbB[sbk]])
                        A(lambda e: e.activation(out=eT[es_][:, c0:512], in_=pb[sbk][:, c0:512], func=AF.Exp, scale=0.125),
                          reads=[pbB[sbk]], writes=[eTB[es_]])
                        if di >= 0:
                            P(lambda e: e.tensor_tensor(out=eT[es_][:, c0:c0 + 128], in0=eT[es_][:, c0:c0 + 128],
                                                        in1=tri[:], op=ALU.mult), reads=[smB], writes=[eTB[es_]])

                        def av(es_=es_, j=j, kt=kt, c0=c0, nk=nk, h=h):
                            T(lambda e: e.matmul(pb[O[j]][:, c0:512], lhsT=vt[:, kt, h * 128:(h + 1) * 128],
                                                 rhs=eT[es_][:, c0:512], start=(kt == 0), stop=(kt == nk - 1)),
                              reads=[eTB[es_], vB[kt]], writes=[pbB[O[j]]], inc=(kt == nk - 1))
                            T(lambda e: e.matmul(pb[Sm[j]][:, c0:512], lhsT=ones_b[:], rhs=eT[es_][:, c0:512],
                                                 start=(kt == 0), stop=(kt == nk - 1)),
                              reads=[eTB[es_], smB], writes=[pbB[Sm[j]]], inc=True)
                        cur.append(av)
                    for f in pend:
                        f()
                    pend = cur
                    if kt == min(5, nk - 1):
                        for fa, fb in fin_pending:
                            fa()
                    if kt == min(9, nk - 1):
                        for fa, fb in fin_pending:
                            fb()
                        fin_pending = []
                for f in pend:
                    f()
                pend = []
                g = gcnt % 2
                gcnt += 1
                V(lambda e: e.tensor_copy(out=osb[g][0][:], in_=pb[O[0]][:, :]), reads=[pbB[O[0]]], writes=[osbB[g][0]])
                V(lambda e: e.tensor_scalar_mul(out=ssb[g][0][:], in0=pb[Sm[0]][:, :], scalar1=2.0 ** -10), reads=[pbB[Sm[0]]], writes=[ssbB[g][0]])
                V(lambda e: e.tensor_copy(out=osb[g][1][:], in_=pb[O[1]][:, :]), reads=[pbB[O[1]]], writes=[osbB[g][1]])
                V(lambda e: e.tensor_scalar_mul(out=ssb[g][1][:], in0=pb[Sm[1]][:, :], scalar1=2.0 ** -10), reads=[pbB[Sm[1]]], writes=[ssbB[g][1]])
                V(lambda e: e.tensor_tensor(out=ot[g][:], in0=osb[g][0][:], in1=ssb[g][1][:], op=ALU.mult),
                  reads=[osbB[g][0], ssbB[g][1]], writes=[otB[g]])
                V(lambda e: e.tensor_tensor(out=osb[g][1][:], in0=osb[g][1][:], in1=ssb[g][0][:], op=ALU.mult),
                  reads=[ssbB[g][0]], writes=[osbB[g][1]])
                V(lambda e: e.scalar_tensor_tensor(out=ot[g][:], in0=osb[g][1][:], scalar=sm[:, 6:7], in1=ot[g][:],
                                                   op0=ALU.mult, op1=ALU.add), reads=[smB, osbB[g][1]], writes=[otB[g]])
                V(lambda e: e.tensor_tensor(out=ssb[g][0][:], in0=ssb[g][0][:], in1=ssb[g][1][:], op=ALU.mult),
                  reads=[ssbB[g][1]], writes=[ssbB[g][0]])
                V(lambda e: e.scalar_tensor_tensor(out=ssb[g][1][:], in0=ssb[g][0][:], scalar=1e-5, in1=ssb[g][0][:],
                                                   op0=ALU.mult, op1=ALU.mult), reads=[ssbB[g][0]], writes=[ssbB[g][1]])

                def fin_a(g=g):
                    nonlocal scnt
                    V(lambda e: e.tensor_tensor(out=osq[g][:], in0=ot[g][:], in1=ot[g][:], op=ALU.mult), reads=[otB[g]], writes=[osqB[g]])
                    sbk = 4 + (scnt % 4)
                    scnt += 1
                    T(lambda e: e.matmul(pb[sbk][:, :], lhsT=cmf[:, 3, :], rhs=osq[g][:], start=True, stop=True),
                      reads=[osqB[g], constB], writes=[pbB[sbk]])
                    V(lambda e: e.scalar_tensor_tensor(out=osq[g][:], in0=pb[sbk][:, :], scalar=1.0 / 128, in1=ssb[g][1][:],
                                                       op0=ALU.mult, op1=ALU.add), reads=[pbB[sbk], ssbB[g][1]], writes=[osqB[g]])

                def fin_b(g=g, h=h, qt=qt, q0=q0):
                    A(lambda e: e.activation(out=osq[g][:], in_=osq[g][:], func=AF.Ln), reads=[osqB[g]], writes=[osqB[g]])
                    A(lambda e: e.activation(out=osq[g][:], in_=osq[g][:], func=AF.Exp, scale=-0.5), writes=[osqB[g]])
                    V(lambda e: e.scalar_tensor_tensor(out=mixT[:, 4 + h, q0:q0 + 512], in0=ot[g][:], scalar=sm[:, 7:8],
                                                       in1=osq[g][:], op0=ALU.mult, op1=ALU.mult),
                      reads=[smB, osqB[g], otB[g]], writes=[mixB[4 + h][qt]])
                fin_pending.append((fin_a, fin_b))
                if gcnt <= 4:
                    sbk = 4 + (scnt % 4)
                    scnt += 1
                    ga_piece(gcnt - 1, sbk)
        for fa, fb in fin_pending:
            fa()
            fb()
        fin_pending = []

        if debug and seq == 0:
            sy.barrier()
            kd = sy.dma_sem("dbg")
            sy.dma("sp", kd, dbg_mix[:, :], mixT[:].rearrange("p c t -> p (c t)"))
            sy.dma("sp", kd, dbg_q[:, :], qT[:].rearrange("p c t -> p (c t)"))
            sy.dma("sp", kd, dbg_k[:, :], kT[:].rearrange("p c t -> p (c t)"))
            sy.dma("sp", kd, dbg_v[:, :], vt[:].rearrange("p c t -> p (c t)"))
        sy.barrier()
        ar.p = region
        wdn = ar.alloc("wdn", [128, NFF, 1024], BF16)
        g2bc = ar.alloc("g2bc", [128, 1024], F32)
        gfbc = ar.alloc("gfbc", [128, 1024], F32)
        x1 = ar.alloc("x1", [128, 2, 4, 1024], F32)
        h2T = ar.alloc("h2T", [128, 8, 512], BF16)
        actT = ar.alloc("actT", [128, NFF, 512], BF16)
        xr = [ar.alloc(f"xr{i}", [128, 1024], F32) for i in range(2)]
        hb2 = [ar.alloc(f"hb2{i}", [128, 1024], BF16) for i in range(2)]
        wus = [ar.alloc(f"wus{i}", [128, 8, 256], BF16) for i in range(3)]
        accg = [ar.alloc(f"accg{i}", [128, 512], F32) for i in range(2)]
        accu = [ar.alloc(f"accu{i}", [128, 512], F32) for i in range(2)]
        sgt = [ar.alloc(f"sgt{i}", [128, 512], F32) for i in range(2)]
        junk = ar.alloc("junk", [128, 1024], BF16)
        wdnB, g2B, gfB = sy.buf("wdn"), sy.buf("g2"), sy.buf("gf")
        x1B = [[sy.buf(f"x1{p}{i}") for i in range(4)] for p in range(2)]
        h2B = [sy.buf(f"h2{i}") for i in range(4)]
        actB = [sy.buf(f"act{i}") for i in range(NFF)]
        xrB = [sy.buf(f"xr{i}") for i in range(2)]
        hb2B = [sy.buf(f"hb2{i}") for i in range(2)]
        wusB = [sy.buf(f"wus{i}") for i in range(3)]
        accgB = [sy.buf(f"accg{i}") for i in range(2)]
        accuB = [sy.buf(f"accu{i}") for i in range(2)]
        sgB = [sy.buf(f"sg{i}") for i in range(2)]
        junkB = sy.buf("junk")
        k_f0 = sy.dma_sem(f"f0_{seq}")
        k_f1 = sy.dma_sem(f"f1_{seq}")
        k_f2 = sy.dma_sem(f"f2_{seq}")
        k_xr = [sy.dma_sem(f"xr{seq}_{i}") for i in range(2)]
        k_wus = [sy.dma_sem(f"wus{seq}_{i}") for i in range(3)]
        k_out = [sy.dma_sem(f"out{seq}_{i}") for i in range(4)]
        out_keys += k_out

        sy.dma("sp", k_f0, g2bc[:], g3_d[1:2, :].partition_broadcast(128), writes=[g2B])
        sy.dma("sp", k_f1, gfbc[:], g3_d[2:3, :].partition_broadcast(128), writes=[gfB])

        def load_xr(n):
            r0 = seq * S + n * 128
            sy.dma("sp", k_xr[n % 2], xr[n % 2][:], x_d[r0:r0 + 128, :], writes=[xrB[n % 2]])

        wu_n = {"n": 0}

        def load_wup(c):
            s3 = wu_n["n"] % 3
            wu_n["n"] += 1
            sy.dma("sp", k_wus[s3], wus[s3][:],
                   wupb_d[c * 256:(c + 1) * 256, :].rearrange("(p a) n -> p (a n)", a=2).rearrange("p (k c) -> p k c", k=8),
                   reads=[wupbB], writes=[wusB[s3]])
            return s3

        load_xr(0)
        load_xr(1)
        wdn_pieces = [(0, 6), (6, 12), (12, 17), (17, NFF)]
        wdnPB = [sy.buf(f"wdnp{i}") for i in range(4)]
        k_wdn = [sy.dma_sem(f"wdn{seq}_{i}") for i in range(4)]
        wdn_of = {}
        for pi_, (c0_, c1_) in enumerate(wdn_pieces):
            for c_ in range(c0_, c1_):
                wdn_of[c_] = pi_

        def load_wdn(pi_):
            c0_, c1_ = wdn_pieces[pi_]
            sy.dma("sp", k_wdn[pi_], wdn[:, c0_:c1_, :],
                   wdnb_d[c0_ * 128:c1_ * 128, :].rearrange("(c p) n -> p c n", p=128), reads=[wdnbB], writes=[wdnPB[pi_]])

        def p3_mm(tl, sub):
            n = tl * 4 + sub
            lc = n * 128
            par = tl % 2
            bp = (4, 5)
            for half in range(2):
                for c in range(8):
                    T(lambda e: e.matmul(pb[bp[half]][:, :], lhsT=mixT[:, c, lc:lc + 128],
                                         rhs=wout[:, c, half * 512:(half + 1) * 512], start=(c == 0), stop=(c == 7)),
                      reads=[mixB[c][tl], woutB], writes=[pbB[bp[half]]], inc=(c == 7))
                V(lambda e: e.tensor_tensor(out=x1[:, par, sub, half * 512:(half + 1) * 512], in0=pb[bp[half]][:, :],
                                            in1=xr[n % 2][:, half * 512:(half + 1) * 512], op=ALU.add),
                  reads=[pbB[bp[half]], xrB[n % 2]], writes=[x1B[par][sub]])
            if n + 2 < 16:
                load_xr(n + 2)
            xs = n % 2
            rstd, sb = rms_stat(2 + xs, x1[:, par, sub, :], junk[:], [x1B[par][sub]], junkB, D, 0)
            V(lambda e: e.scalar_tensor_tensor(out=hb2[xs][:], in0=x1[:, par, sub, :], scalar=rstd, in1=g2bc[:],
                                               op0=ALU.mult, op1=ALU.mult),
              reads=[x1B[par][sub], sb, g2B], writes=[hb2B[xs]], strict=[sb])

        def p3_tr(tl, sub):
            n = tl * 4 + sub
            xs = n % 2
            for kc in range(8):
                T(lambda e: e.transpose(out=pbT[:, kc * 128:(kc + 1) * 128], in_=hb2[xs][:, kc * 128:(kc + 1) * 128],
                                        identity=ident[:]),
                  reads=[hb2B[xs], smB], writes=[pbTB], inc=(kc == 7))
            A(lambda e: e.activation(out=h2T[:, :, sub * 128:(sub + 1) * 128],
                                     in_=pbT[:].rearrange("p (k c) -> p k c", k=8), func=AF.Copy),
              reads=[pbTB], writes=[h2B[sub]])

        def p3_pieces(tl):
            return [lambda: p3_mm(tl, 0), lambda: p3_mm(tl, 1), lambda: p3_tr(tl, 0), lambda: p3_mm(tl, 2),
                    lambda: p3_tr(tl, 1), lambda: p3_mm(tl, 3), lambda: p3_tr(tl, 2), lambda: p3_tr(tl, 3)]

        def ffn_up(tl):
            slots = [load_wup(0), load_wup(1)]
            for c in range(NFF):
                if c + 2 < NFF:
                    slots.append(load_wup(c + 2))
                if tl == 0 and c in (2, 6, 10, 14):
                    load_wdn((c - 2) // 4)
                s3 = slots[c]
                a2 = c % 2
                banks = (next_bank(6), next_bank(6))
                for gu in range(2):
                    bi = banks[gu]
                    for kc in range(8):
                        T(lambda e: e.matmul(pb[bi][:, :], lhsT=wus[s3][:, kc, gu * 128:(gu + 1) * 128], rhs=h2T[:, kc, :],
                                             start=(kc == 0), stop=(kc == 7)),
                          reads=[wusB[s3]] + h2B, writes=[pbB[bi]], inc=(kc == 7))
                for gu in range(2):
                    bi = banks[gu]
                    cc = c + gu * NFF
                    acc, accB_ = (accg[a2], accgB[a2]) if gu == 0 else (accu[a2], accuB[a2])
                    A(lambda e: e.activation(out=acc[:], in_=pb[bi][:, :], func=AF.Identity, scale=CW(2, cc), bias=CB(cc)),
                      reads=[pbB[bi], constB], writes=[accB_])
                    V(lambda e: e.scalar_tensor_tensor(out=acc[:, 1:512], in0=pb[bi][:, 0:511], scalar=CW(1, cc),
                                                       in1=acc[:, 1:512], op0=ALU.mult, op1=ALU.add),
                      reads=[pbB[bi]], writes=[accB_])
                    V(lambda e: e.scalar_tensor_tensor(out=acc[:, 2:512], in0=pb[bi][:, 0:510], scalar=CW(0, cc),
                                                       in1=acc[:, 2:512], op0=ALU.mult, op1=ALU.add),
                      reads=[pbB[bi]], writes=[accB_])
                    if tl > 0:
                        hp = (tl - 1) % 2
                        V(lambda e: e.scalar_tensor_tensor(out=acc[:, 0:1], in0=hal[:, hp, cc, 1:2], scalar=CW(1, cc),
                                                           in1=acc[:, 0:1], op0=ALU.mult, op1=ALU.add),
                          reads=[halB[hp][cc]], writes=[accB_])
                        V(lambda e: e.scalar_tensor_tensor(out=acc[:, 0:2], in0=hal[:, hp, cc, 0:2], scalar=CW(0, cc),
                                                           in1=acc[:, 0:2], op0=ALU.mult, op1=ALU.add),
                          reads=[halB[hp][cc]], writes=[accB_])
                    if tl < 3:
                        A(lambda e: e.activation(out=hal[:, tl % 2, cc, :], in_=pb[bi][:, 510:512], func=AF.Copy),
                          reads=[pbB[bi]], writes=[halB[tl % 2][cc]])
                A(lambda e: e.activation(out=sgt[a2][:], in_=accg[a2][:], func=AF.Silu), reads=[accgB[a2]], writes=[sgB[a2]])
                P(lambda e: e.tensor_tensor(out=actT[:, c, :], in0=sgt[a2][:], in1=accu[a2][:], op=ALU.mult),
                  reads=[sgB[a2], accuB[a2]], writes=[actB[c]])

        def ffn_down_half(tl, sub, half):
            n = tl * 4 + sub
            r0 = seq * S + n * 128
            par = tl % 2
            bp = (0, 1) if sub % 2 == 0 else (2, 3)
            for c in range(NFF):
                T(lambda e: e.matmul(pb[bp[half]][:, :], lhsT=actT[:, c, sub * 128:(sub + 1) * 128],
                                     rhs=wdn[:, c, half * 512:(half + 1) * 512], start=(c == 0), stop=(c == NFF - 1)),
                  reads=[actB[c], wdnPB[wdn_of[c]]], writes=[pbB[bp[half]]], inc=(c == NFF - 1))
            V(lambda e: e.tensor_tensor(out=x1[:, par, sub, half * 512:(half + 1) * 512], in0=pb[bp[half]][:, :],
                                        in1=x1[:, par, sub, half * 512:(half + 1) * 512], op=ALU.add),
              reads=[pbB[bp[half]]], writes=[x1B[par][sub]])
            if half == 1:
                rstd, sb = rms_stat(4 + (n % 2), x1[:, par, sub, :], junk[:], [x1B[par][sub]], junkB, D, 0)
                V(lambda e: e.scalar_tensor_tensor(out=x1[:, par, sub, :], in0=x1[:, par, sub, :], scalar=rstd, in1=gfbc[:],
                                                   op0=ALU.mult, op1=ALU.mult),
                  reads=[sb, gfB], writes=[x1B[par][sub]], strict=[sb])
                sy.dma("pool", k_out[sub], out_d[r0:r0 + 128, :], x1[:, par, sub, :], reads=[x1B[par][sub]])

        for f in p3_pieces(0):
            f()
        for tl in range(4):
            ffn_up(tl)
            pieces = p3_pieces(tl + 1) if tl < 3 else []
            pi = 0
            for sub in range(4):
                for half in range(2):
                    ffn_down_half(tl, sub, half)
                    if pi < len(pieces):
                        pieces[pi]()
                        pi += 1

    sy.final_wait("pool", out_keys)
    sy.barrier()
    es.close()
    return nc


_NC = None


def _host_consts():
    inv = (10000.0 ** (-(np.arange(0, 64, 2, dtype=np.float32) / np.float32(64)))).astype(np.float32)
    ang = (np.arange(S, dtype=np.float32)[:, None] * inv[None, :]).astype(np.float32)
    cos = np.cos(ang).astype(np.float32).T
    sin = np.sin(ang).astype(np.float32).T
    cs = np.zeros((2, 128, S), np.float32)
    for p in range(128):
        d = p % 64
        i = d % 32
        cs[0, p] = cos[i]
        cs[1, p] = -sin[i] if d < 32 else sin[i]
    cm = np.zeros((4, 128, 128), np.float32)
    cm[0] = np.eye(128, dtype=np.float32)
    for p in range(128):
        d = p % 64
        partner = p + 32 if d < 32 else p - 32
        cm[1, partner, p] = 1.0
    kk = np.arange(128)[:, None]
    qq = np.arange(128)[None, :]
    cm[2] = (kk <= qq).astype(np.float32)
    cm[3] = 1.0
    return cs.reshape(256, S), cm.reshape(512, 128)


def kernel(x, norm1_g, w_in, short_conv_w, mix_a_norm_g, lambda_q1, lambda_k1, lambda_q2, lambda_k2,
           subln_g, w_out, norm2_g, w_up, ffn_conv_w, ffn_conv_b, w_down, final_g):
    global _NC
    f = lambda a: np.ascontiguousarray(np.asarray(a, dtype=np.float32))
    x = f(x)
    w_in0 = f(w_in)[0]
    order = []
    for i in range(4):
        order += [512 + 128 * i, 1024 + 128 * i, 128 * i]
    order += [1536 + 128 * h for h in range(4)] + [2048 + 128 * h for h in range(4)]
    w_in_fm = np.stack([w_in0[:, c:c + 128].reshape(8, 128, 128).transpose(1, 0, 2) for c in order])
    w_in_fm = np.ascontiguousarray(w_in_fm).reshape(20 * 128, 1024)
    w_v = np.ascontiguousarray(w_in0[:, 2560:3072])
    w_up0 = f(w_up)[0]
    chunks = []
    for c in range(NFF):
        gcol = w_up0[:, c * 128:(c + 1) * 128].reshape(8, 128, 128)
        ucol = w_up0[:, DFF + c * 128:DFF + (c + 1) * 128].reshape(8, 128, 128)
        chunks.append(np.concatenate([gcol, ucol], axis=2).transpose(1, 0, 2))
    w_up_fm = np.ascontiguousarray(np.stack(chunks)).reshape(NFF * 128 * 2, 1024)
    g3 = np.ascontiguousarray(np.stack([f(norm1_g)[0], f(norm2_g)[0], f(final_g)]))
    lamv = np.concatenate([f(lambda_q1)[0], f(lambda_k1)[0], f(lambda_q2)[0], f(lambda_k2)[0]])[None, :]
    cols = np.zeros((128, NCOLS), np.float32)
    fw = f(ffn_conv_w)[0]
    for j in range(3):
        cols[:, j * 44:(j + 1) * 44] = fw[j].reshape(44, 128).T
    cols[:, 132:176] = f(ffn_conv_b)[0].reshape(44, 128).T
    sw = f(short_conv_w)[0]
    for j in range(3):
        cols[:, 176 + 4 * j:180 + 4 * j] = sw[j].reshape(4, 128).T
    cols[:, 188:192] = f(mix_a_norm_g)[0].reshape(4, 128).T
    cols[:, 192] = f(subln_g)[0]
    cs, cm = _host_consts()
    shared = {
        "w_in_fm": w_in_fm, "w_v": w_v, "w_out": f(w_out)[0], "w_up_fm": w_up_fm, "w_down": f(w_down)[0],
        "g3": g3, "lamv": np.ascontiguousarray(lamv), "cols": cols, "cs": cs, "cmat": cm,
    }
    xs = x.reshape(NCORES, TOK, D)
    in_maps = [dict(shared, x=np.ascontiguousarray(xs[c])) for c in range(NCORES)]
    if _NC is None:
        _NC = build_nc()
    res = run_bass_kernel_spmd(_NC, in_maps, core_ids=list(range(NCORES)))
    out = np.stack([np.asarray(r["out"]) for r in res.results]).reshape(16, S, D)
    return out.astype(np.float32)
```

```python
import numpy as np
import concourse.bass as bass
import concourse.mybir as mybir
from concourse.bass_utils import run_bass_kernel_spmd

F32 = mybir.dt.float32
BF16 = mybir.dt.bfloat16
AF = mybir.ActivationFunctionType
ALU = mybir.AluOpType
AX = mybir.AxisListType

NCORES = 8
D = 1024
S = 2048
NSEQ = 2
TOK = NSEQ * S
DFF = 2816
NFF = DFF // 128
SBUF_BASE = 16512
SBUF_LIMIT = 229344
NCOLS = 193


class Buf:
    __slots__ = ("name", "w", "r")

    def __init__(self, name):
        self.name = name
        self.w = None
        self.r = {}


class Sync:
    def __init__(self, nc, es):
        self.nc = nc
        self.es = es
        self.engs = {"pe": nc.tensor, "act": nc.scalar, "dve": nc.vector, "pool": nc.gpsimd, "sp": nc.sync}
        self.semh = {}
        self.cnt = {}
        for e in self.engs:
            self.semh[e] = es.enter_context(nc.semaphore("sem_" + e))
            self.cnt[e] = 0
        self.seen = {e: {} for e in self.engs}
        self.bufs = []
        self.dma_keys = []
        self.shared_keys = set()

    def buf(self, name):
        b = Buf(name)
        self.bufs.append(b)
        return b

    def dma_sem(self, name):
        key = "d_" + name
        self.semh[key] = self.es.enter_context(self.nc.semaphore("sem_" + key))
        self.cnt[key] = 0
        self.dma_keys.append(key)
        return key

    def _waits(self, e, reads, writes, same_ok, strict=()):
        waits = {}
        for b in strict:
            if b.w is not None and waits.get(b.w[0], 0) < b.w[1]:
                waits[b.w[0]] = b.w[1]

        def need(ev):
            if ev is None:
                return
            k, v = ev
            if same_ok and k == e:
                return
            if waits.get(k, 0) < v:
                waits[k] = v

        for b in reads:
            need(b.w)
        for b in writes:
            need(b.w)
            for k, v in b.r.items():
                need((k, v))
        eng = self.engs[e]
        for k, v in waits.items():
            if self.seen[e].get(k, 0) >= v:
                continue
            eng.wait_ge(self.semh[k], v)
            self.seen[e][k] = v

    def op(self, e, fn, reads=(), writes=(), inc=True, strict=()):
        self._waits(e, reads, writes, same_ok=True, strict=strict)
        ins = fn(self.engs[e])
        if inc:
            self.cnt[e] += 1
            ins.then_inc(self.semh[e], 1)
            v = self.cnt[e]
        else:
            v = self.cnt[e] + 1
        for b in reads:
            if b.r.get(e, 0) < v:
                b.r[e] = v
        for b in writes:
            b.w = (e, v)
            b.r = {}

    def dma(self, q, key, out, in_, reads=(), writes=(), after=()):
        self._waits(q, list(reads) + list(after), writes, same_ok=False)
        if self.cnt[key] and self.seen[q].get(key, 0) < self.cnt[key] and key not in self.shared_keys:
            self.engs[q].wait_ge(self.semh[key], self.cnt[key])
            self.seen[q][key] = self.cnt[key]
        ins = self.engs[q].dma_start(out=out, in_=in_)
        self.cnt[key] += 16
        ins.then_inc(self.semh[key], 16)
        v = self.cnt[key]
        for b in reads:
            if b.r.get(key, 0) < v:
                b.r[key] = v
        for b in writes:
            b.w = (key, v)
            b.r = {}

    def barrier(self):
        keys = list(self.engs.keys()) + self.dma_keys
        for e in self.engs:
            for k in keys:
                if k == e:
                    continue
                v = self.cnt[k]
                if v == 0 or self.seen[e].get(k, 0) >= v:
                    continue
                self.engs[e].wait_ge(self.semh[k], v)
                self.seen[e][k] = v
        for b in self.bufs:
            b.w = None
            b.r = {}

    def final_wait(self, e, keys):
        for k in keys:
            v = self.cnt[k]
            if v:
                self.engs[e].wait_ge(self.semh[k], v)


class Arena:
    def __init__(self, nc):
        self.nc = nc
        self.p = SBUF_BASE
        self.n = 0

    def alloc(self, name, shape, dtype):
        esz = 2 if dtype == BF16 else 4
        per = esz
        for d in shape[1:]:
            per *= d
        off = (self.p + 31) // 32 * 32
        assert off + per <= SBUF_LIMIT, (name, off, per)
        self.n += 1
        t = self.nc.alloc_sbuf_tensor_at(f"{name}_{self.n}", list(shape), dtype, offset=off)
        self.p = off + per
        return t


def build_nc(debug=False):
    from contextlib import ExitStack

    nc = bass.Bass("TRN2", target_bir_lowering=False)
    dt = nc.dram_tensor
    x_d = dt("x", [TOK, D], F32, kind="ExternalInput").ap()
    win_d = dt("w_in_fm", [20 * 128, 1024], F32, kind="ExternalInput").ap()
    wv_d = dt("w_v", [1024, 512], F32, kind="ExternalInput").ap()
    wout_d = dt("w_out", [1024, 1024], F32, kind="ExternalInput").ap()
    wup_d = dt("w_up_fm", [NFF * 128 * 2, 1024], F32, kind="ExternalInput").ap()
    wdn_d = dt("w_down", [DFF, 1024], F32, kind="ExternalInput").ap()
    g3_d = dt("g3", [3, 1024], F32, kind="ExternalInput").ap()
    lam_d = dt("lamv", [1, 256], F32, kind="ExternalInput").ap()
    cols_d = dt("cols", [128, NCOLS], F32, kind="ExternalInput").ap()
    cs_d = dt("cs", [2 * 128, S], F32, kind="ExternalInput").ap()
    cm_d = dt("cmat", [4 * 128, 128], F32, kind="ExternalInput").ap()
    out_d = dt("out", [TOK, D], F32, kind="ExternalOutput").ap()
    winb_d = dt("w_in_b", [20 * 128, 1024], BF16, kind="Internal").ap()
    wvb_d = dt("w_v_b", [1024, 512], BF16, kind="Internal").ap()
    woutb_d = dt("w_out_b", [1024, 1024], BF16, kind="Internal").ap()
    wupb_d = dt("w_up_b", [NFF * 128 * 2, 1024], BF16, kind="Internal").ap()
    wdnb_d = dt("w_down_b", [DFF, 1024], BF16, kind="Internal").ap()

    if debug:
        dbg_mix = dt("dbg_mix", [128, 8 * S], BF16, kind="ExternalOutput").ap()
        dbg_q = dt("dbg_q", [128, 4 * S], BF16, kind="ExternalOutput").ap()
        dbg_k = dt("dbg_k", [128, 4 * S], BF16, kind="ExternalOutput").ap()
        dbg_v = dt("dbg_v", [128, 16 * 512], BF16, kind="ExternalOutput").ap()
    es = ExitStack()
    sy = Sync(nc, es)
    ar = Arena(nc)

    pb = [nc.alloc_psum_tensor(f"pb{i}", [128, 512], F32) for i in range(8)]
    pbT = pb[7][:, :].bitcast(BF16)
    pbB = [sy.buf(f"pb{i}") for i in range(8)]
    pbTB = pbB[7]

    mixT = ar.alloc("mixT", [128, 8, S], BF16)
    wout = ar.alloc("wout", [128, 8, 1024], BF16)
    cols = ar.alloc("cols", [128, NCOLS], F32)
    cmf = ar.alloc("cmf", [128, 4, 128], F32)
    ident = ar.alloc("ident", [128, 128], BF16)
    pswap = ar.alloc("pswap", [128, 128], BF16)
    tri = ar.alloc("tri", [128, 128], BF16)
    ones_b = ar.alloc("ones_b", [128, 128], BF16)
    lamt = ar.alloc("lamt", [128, 256], F32)
    sm = ar.alloc("sm", [128, 16], F32)
    hal = ar.alloc("hal", [128, 2, 2 * NFF, 2], F32)
    stat = ar.alloc("stat", [128, 8, 4], F32)
    region = ar.p

    mixB = [[sy.buf(f"mix{c}_{t}") for t in range(4)] for c in range(8)]
    woutB = sy.buf("wout")
    constB = sy.buf("const")
    smB = sy.buf("sm")
    halB = [[sy.buf(f"hal{p}_{c}") for c in range(2 * NFF)] for p in range(2)]
    statB = [sy.buf(f"stat{i}") for i in range(8)]

    winbB, wvbB, woutbB, wupbB, wdnbB = (sy.buf(n) for n in ("winb", "wvb", "woutb", "wupb", "wdnb"))

    k_c1 = sy.dma_sem("cast_win")
    k_c2 = sy.dma_sem("cast_wv")
    k_c3 = sy.dma_sem("cast_wout")
    k_c4 = sy.dma_sem("cast_wup")
    k_c5 = sy.dma_sem("cast_wdn")
    sy.dma("pool", k_c2, wvb_d[:, :], wv_d[:, :], writes=[wvbB])

    k_const = sy.dma_sem("const")
    sy.shared_keys.add(k_const)
    sy.dma("sp", k_const, cols[:], cols_d[:, :])
    sy.dma("sp", k_const, cmf[:], cm_d.rearrange("(m p) c -> p m c", p=128))
    sy.dma("sp", k_const, lamt[:], lam_d[0:1, :].partition_broadcast(128))
    constB.w = (k_const, sy.cnt[k_const])
    k_wout = sy.dma_sem("wout")

    V = lambda f, **kw: sy.op("dve", f, **kw)
    A = lambda f, **kw: sy.op("act", f, **kw)
    P = lambda f, **kw: sy.op("pool", f, **kw)
    T = lambda f, **kw: sy.op("pe", f, **kw)

    V(lambda e: e.tensor_copy(out=ident[:], in_=cmf[:, 0, :]), reads=[constB], writes=[smB])
    V(lambda e: e.tensor_copy(out=pswap[:], in_=cmf[:, 1, :]), reads=[constB], writes=[smB])
    V(lambda e: e.tensor_copy(out=tri[:], in_=cmf[:, 2, :]), reads=[constB], writes=[smB])
    V(lambda e: e.tensor_copy(out=ones_b[:], in_=cmf[:, 3, :]), reads=[constB], writes=[smB])
    V(lambda e: e.memset(sm[:, 0:1], 1e-6), writes=[smB])
    V(lambda e: e.memset(sm[:, 1:2], 1e-5), writes=[smB])
    V(lambda e: e.memset(hal[:], 0.0), writes=halB[0] + halB[1])
    lt = ar.alloc("lamtmp", [128, 128], F32)
    V(lambda e: e.tensor_tensor(out=lt[:, 0:64], in0=lamt[:, 0:64], in1=lamt[:, 64:128], op=ALU.mult), reads=[constB], writes=[smB])
    V(lambda e: e.tensor_tensor(out=lt[:, 64:128], in0=lamt[:, 128:192], in1=lamt[:, 192:256], op=ALU.mult), writes=[smB])
    V(lambda e: e.reduce_sum(out=sm[:, 2:3], in_=lt[:, 0:64], axis=AX.X), writes=[smB], strict=[smB])
    V(lambda e: e.reduce_sum(out=sm[:, 3:4], in_=lt[:, 64:128], axis=AX.X), writes=[smB], strict=[smB])
    A(lambda e: e.activation(out=sm[:, 4:6], in_=sm[:, 2:4], func=AF.Exp), reads=[smB], writes=[smB])
    V(lambda e: e.tensor_sub(out=sm[:, 6:7], in0=sm[:, 5:6], in1=sm[:, 4:5]), reads=[smB], writes=[smB])
    V(lambda e: e.tensor_scalar_add(out=sm[:, 6:7], in0=sm[:, 6:7], scalar1=-0.2), writes=[smB], strict=[smB])
    V(lambda e: e.tensor_scalar_mul(out=sm[:, 7:8], in0=cols[:, 192:193], scalar1=0.8), reads=[constB], writes=[smB])
    region = ar.p

    CW = lambda j, c: cols[:, j * 44 + c:j * 44 + c + 1]
    CB = lambda c: cols[:, 132 + c:133 + c]
    SCW = lambda j, i: cols[:, 176 + j * 4 + i:177 + j * 4 + i]
    GA = lambda i: cols[:, 188 + i:189 + i]

    rot = {"i": 0}

    def next_bank(n=6):
        i = rot["i"] % n
        rot["i"] += 1
        return i

    def rms_stat(slot, src_ap, junk_ap, srcB, junkB, n_feat, eps_col):
        sb = statB[slot]
        A(lambda e: e.activation(out=junk_ap, in_=src_ap, func=AF.Square, accum_out=stat[:, slot, 0:1]),
          reads=srcB, writes=[junkB, sb])
        A(lambda e: e.activation(out=stat[:, slot, 1:2], in_=stat[:, slot, 0:1], func=AF.Sqrt,
                                 scale=1.0 / n_feat, bias=sm[:, eps_col:eps_col + 1]), reads=[smB], writes=[sb], strict=[sb])
        V(lambda e: e.reciprocal(out=stat[:, slot, 2:3], in_=stat[:, slot, 1:2]), reads=[sb], writes=[sb])
        return stat[:, slot, 2:3], sb

    out_keys = []

    for seq in range(NSEQ):
        if seq > 0:
            sy.barrier()
        ar.p = region
        qT = ar.alloc("qT", [128, 4, S], BF16)
        kT = ar.alloc("kT", [128, 4, S], BF16)
        vt = ar.alloc("vt", [128, 16, 512], BF16)
        hT = ar.alloc("hT", [128, 8, S], BF16)
        p2_base = ar.p - 32768
        p01_base = ar.p
        wv = ar.alloc("wv", [128, 8, 512], BF16)
        g1bc = ar.alloc("g1bc", [128, 1024], F32)
        xt = [ar.alloc(f"xt{i}", [128, 1024], F32) for i in range(4)]
        hb = [ar.alloc(f"hb{i}", [128, 1024], BF16) for i in range(2)]
        ar_save = ar.p
        ar.p = p01_base + 65536
        wsl = [ar.alloc(f"wsl{i}", [128, 8, 128], BF16) for i in range(3)]
        wsl_end = ar.p
        ar.p = ar_save
        wslB = [sy.buf(f"wsl{i}") for i in range(3)]
        k_wsl = [sy.dma_sem(f"wsl{seq}_{i}") for i in range(3)]

        def load_win(ci):
            s3 = ci % 3
            sy.dma("sp", k_wsl[s3], wsl[s3][:],
                   winb_d[ci * 128:(ci + 1) * 128, :].rearrange("p (k c) -> p k c", k=8),
                   reads=[winbB], writes=[wslB[s3]])
        qB = [[sy.buf(f"q{h}_{t}") for t in range(4)] for h in range(4)]
        kB = [[sy.buf(f"k{h}_{t}") for t in range(4)] for h in range(4)]
        vB = [sy.buf(f"v{t}") for t in range(16)]
        hTB = [sy.buf(f"hT{t}") for t in range(16)]
        wvB, g1B = sy.buf("wv"), sy.buf("g1")
        xtB = [sy.buf(f"xt{i}") for i in range(4)]
        hbB = [sy.buf(f"hb{i}") for i in range(2)]
        k_xt = [sy.dma_sem(f"xt{seq}_{i}") for i in range(4)]
        k_m0 = sy.dma_sem(f"m0_{seq}")
        k_m1 = sy.dma_sem(f"m1_{seq}")

        sy.dma("sp", k_m0, g1bc[:], g3_d[0:1, :].partition_broadcast(128), writes=[g1B])

        def load_x(i):
            r0 = seq * S + i * 128
            sy.dma("sp", k_xt[i % 4], xt[i % 4][:], x_d[r0:r0 + 128, :], writes=[xtB[i % 4]])

        junk0 = ar.alloc("junk0", [128, 1024], BF16)
        junk0B = sy.buf("junk0")
        for i0 in range(4):
            load_x(i0)
        sy.dma("sp", k_m1, wv[:], wvb_d.rearrange("(c p) n -> p c n", p=128), reads=[wvbB], writes=[wvB])
        if seq == 0:
            sy.dma("pool", k_c1, winb_d[:, :], win_d[:, :], after=[xtB[0], xtB[1], xtB[2], xtB[3]], writes=[winbB])
        st = {0: rms_stat(0, xt[0][:], junk0[:], [xtB[0]], junk0B, D, 0)}
        for i in range(16):
            sl = i % 2
            x4 = i % 4
            rstd, sb = st[i]
            V(lambda e: e.scalar_tensor_tensor(out=hb[sl][:], in0=xt[x4][:], scalar=rstd, in1=g1bc[:],
                                               op0=ALU.mult, op1=ALU.mult),
              reads=[xtB[x4], sb, g1B], writes=[hbB[sl]], strict=[sb])
            if i + 4 < 16:
                load_x(i + 4)
            if i + 1 < 16:
                st[i + 1] = rms_stat((i + 1) % 2, xt[(i + 1) % 4][:], junk0[:], [xtB[(i + 1) % 4]], junk0B, D, 0)
            for kc in range(8):
                T(lambda e: e.transpose(out=pbT[:, kc * 128:(kc + 1) * 128], in_=hb[sl][:, kc * 128:(kc + 1) * 128],
                                        identity=ident[:]),
                  reads=[hbB[sl], smB], writes=[pbTB], inc=(kc == 7))
            V(lambda e: e.tensor_copy(out=hT[:, :, i * 128:(i + 1) * 128],
                                      in_=pbT[:].rearrange("p (k c) -> p k c", k=8)),
              reads=[pbTB], writes=[hTB[i]])

            def vproj(i=i):
                bi = next_bank(2)
                for kc in range(8):
                    T(lambda e: e.matmul(pb[bi][:, :], lhsT=hT[:, kc, i * 128:(i + 1) * 128], rhs=wv[:, kc, :],
                                         start=(kc == 0), stop=(kc == 7)),
                      reads=[hTB[i], wvB], writes=[pbB[bi]], inc=(kc == 7))
                A(lambda e: e.activation(out=vt[:, i, :], in_=pb[bi][:, :], func=AF.Copy), reads=[pbB[bi]], writes=[vB[i]])
            if i > 0:
                vproj(i - 1)
            if i == 15:
                vproj(15)

        load_win(0)
        load_win(1)
        sy.barrier()
        ar.p = p01_base
        cs = ar.alloc("cs", [128, 2, S], F32)
        gcs = [ar.alloc(f"gcs{i}", [128, 512], F32) for i in range(4)]
        cx = [ar.alloc(f"cx{i}", [128, 516], F32) for i in range(2)]
        yv = [ar.alloc(f"yv{i}", [128, 512], F32) for i in range(4)]
        yaf = [ar.alloc(f"yaf{i}", [128, 512], F32) for i in range(2)]
        sqt = [ar.alloc(f"sqt{i}", [128, 512], F32) for i in range(1)] * 2
        accss = ar.alloc("accss", [128, S], F32)
        stdt = [ar.alloc(f"stdt{i}", [128, 512], F32) for i in range(1)] * 2
        qraw = [ar.alloc(f"qraw{i}", [128, 512], BF16) for i in range(2)]
        rt = [ar.alloc(f"rt{i}", [128, 512], F32) for i in range(2)]
        ru = [ar.alloc(f"ru{i}", [128, 512], F32) for i in range(2)]
        assert ar.p <= p01_base + 65536
        p1_end = wsl_end
        csB = sy.buf("cs")
        gcsB = [sy.buf(f"gcs{i}") for i in range(4)]
        cxB = [sy.buf(f"cx{i}") for i in range(2)]
        yvB = [sy.buf(f"yv{i}") for i in range(4)]
        yafB = [sy.buf(f"yaf{i}") for i in range(2)]
        sqtB = [sy.buf(f"sqt{i}") for i in range(1)] * 2
        accB = [sy.buf(f"acc{i}") for i in range(4)]
        stdB = [sy.buf(f"std{i}") for i in range(1)] * 2
        qrawB = [sy.buf(f"qraw{i}") for i in range(2)]
        rtB = [sy.buf(f"rt{i}") for i in range(2)]
        ruB = [sy.buf(f"ru{i}") for i in range(2)]
        k_cs = sy.dma_sem(f"cs{seq}")

        if seq == 0:
            sy.dma("pool", k_c3, woutb_d[:, :], wout_d[:, :], writes=[woutbB])
        sy.dma("sp", k_cs, cs[:], cs_d.rearrange("(m p) t -> p m t", p=128), writes=[csB])
        V(lambda e: e.memset(cx[0][:, 0:2], 0.0), writes=[cxB[0]])
        V(lambda e: e.memset(cx[1][:, 0:2], 0.0), writes=[cxB[1]])

        deferred = []
        cnt = {"gc": 0, "cx": 0, "ya": 0, "qr": 0}
        for ci in range(20):
            if ci + 2 < 20:
                load_win(ci + 2)
            s3 = ci % 3
            for tt in range(4):
                bi = next_bank(6)
                c0 = tt * 512
                for kc in range(8):
                    T(lambda e: e.matmul(pb[bi][:, :], lhsT=wsl[s3][:, kc, :], rhs=hT[:, kc, c0:c0 + 512],
                                         start=(kc == 0), stop=(kc == 7)),
                      reads=[wslB[s3]], writes=[pbB[bi]], inc=(kc == 7))
                for f in deferred:
                    f()
                deferred = []
                if ci < 12:
                    i, typ = ci // 3, ci % 3
                    if typ == 0:
                        g = tt
                        A(lambda e: e.activation(out=gcs[g][:], in_=pb[bi][:, :], func=AF.Copy),
                          reads=[pbB[bi]], writes=[gcsB[g]])
                    elif typ == 1:
                        g = tt
                        c = cnt["cx"] % 2
                        cnt["cx"] += 1
                        if tt == 0:
                            P(lambda e: e.memset(cx[c][:, 0:2], 0.0), writes=[cxB[c]])
                        V(lambda e: e.tensor_tensor(out=cx[c][:, 2:514], in0=pb[bi][:, :], in1=gcs[g][:], op=ALU.mult),
                          reads=[pbB[bi], gcsB[g]], writes=[cxB[c]])
                        if tt < 3:
                            P(lambda e: e.tensor_copy(out=cx[1 - c][:, 0:2], in_=cx[c][:, 512:514]),
                              reads=[cxB[c]], writes=[cxB[1 - c]])
                        A(lambda e: e.activation(out=yv[tt][:], in_=cx[c][:, 2:514], func=AF.Copy, scale=SCW(2, i)),
                          reads=[cxB[c], constB], writes=[yvB[tt]])
                        V(lambda e: e.scalar_tensor_tensor(out=yv[tt][:], in0=cx[c][:, 1:513], scalar=SCW(1, i),
                                                           in1=yv[tt][:], op0=ALU.mult, op1=ALU.add),
                          reads=[cxB[c]], writes=[yvB[tt]])
                        V(lambda e: e.scalar_tensor_tensor(out=yv[tt][:], in0=cx[c][:, 0:512], scalar=SCW(0, i),
                                                           in1=yv[tt][:], op0=ALU.mult, op1=ALU.add),
                          reads=[cxB[c]], writes=[yvB[tt]])
                    else:
                        a = cnt["ya"] % 2
                        cnt["ya"] += 1
                        V(lambda e: e.tensor_tensor(out=yaf[a][:], in0=pb[bi][:, :], in1=yv[tt][:], op=ALU.mult),
                          reads=[pbB[bi], yvB[tt]], writes=[yafB[a]])
                        A(lambda e: e.activation(out=mixT[:, i, c0:c0 + 512], in_=yaf[a][:], func=AF.Copy),
                          reads=[yafB[a]], writes=[mixB[i][tt]])
                        if i == 0:
                            A(lambda e: e.activation(out=accss[:, c0:c0 + 512], in_=yaf[a][:], func=AF.Square),
                              reads=[yafB[a]], writes=[accB[tt]])
                        else:
                            A(lambda e: e.activation(out=sqt[a][:], in_=yaf[a][:], func=AF.Square),
                              reads=[yafB[a]], writes=[sqtB[a]])
                            P(lambda e: e.tensor_tensor(out=accss[:, c0:c0 + 512], in0=accss[:, c0:c0 + 512],
                                                        in1=sqt[a][:], op=ALU.add),
                              reads=[sqtB[a]], writes=[accB[tt]])
                else:
                    h = (ci - 12) % 4
                    dstT, dstB = (qT, qB) if ci < 16 else (kT, kB)
                    r = cnt["qr"] % 2
                    cnt["qr"] += 1
                    A(lambda e: e.activation(out=qraw[r][:], in_=pb[bi][:, :], func=AF.Copy),
                      reads=[pbB[bi]], writes=[qrawB[r]])
                    V(lambda e: e.tensor_tensor(out=rt[r][:], in0=pb[bi][:, :], in1=cs[:, 0, c0:c0 + 512], op=ALU.mult),
                      reads=[pbB[bi], csB], writes=[rtB[r]])

                    def rope_tail(r=r, h=h, tt=tt, c0=c0, dstT=dstT, dstB=dstB):
                        b2 = next_bank(6)
                        T(lambda e: e.matmul(pb[b2][:, :], lhsT=pswap[:], rhs=qraw[r][:], start=True, stop=True),
                          reads=[qrawB[r], smB], writes=[pbB[b2]])
                        V(lambda e: e.tensor_tensor(out=ru[r][:], in0=pb[b2][:, :], in1=cs[:, 1, c0:c0 + 512], op=ALU.mult),
                          reads=[pbB[b2], csB], writes=[ruB[r]])
                        P(lambda e: e.tensor_tensor(out=dstT[:, h, c0:c0 + 512], in0=rt[r][:], in1=ru[r][:], op=ALU.add),
                          reads=[rtB[r], ruB[r]], writes=[dstB[h][tt]])

                    deferred.append(rope_tail)
            if ci == 11:
                def ga_piece(tt, bank):
                    c0 = tt * 512
                    T(lambda e: e.matmul(pb[bank][:, :], lhsT=cmf[:, 3, :], rhs=accss[:, c0:c0 + 512], start=True, stop=True),
                      reads=[constB], writes=[pbB[bank]])
                    A(lambda e: e.activation(out=stdt[0][:], in_=pb[bank][:, :], func=AF.Ln, scale=1.0 / 512,
                                             bias=sm[:, 0:1]), reads=[pbB[bank], smB], writes=[stdB[0]])
                    A(lambda e: e.activation(out=stdt[0][:], in_=stdt[0][:], func=AF.Exp, scale=-0.5), writes=[stdB[0]])
                    for i in range(4):
                        V(lambda e: e.scalar_tensor_tensor(out=mixT[:, i, c0:c0 + 512], in0=mixT[:, i, c0:c0 + 512],
                                                           scalar=GA(i), in1=stdt[0][:], op0=ALU.mult, op1=ALU.mult),
                          reads=[stdB[0], constB], writes=[mixB[i][tt]])
        for f in deferred:
            f()
        deferred = []

        sy.barrier()
        ar.p = p2_base
        eT = [ar.alloc(f"eT{i}", [128, 512], BF16) for i in range(6)]
        ssb = [[ar.alloc(f"ssb{g}{j}", [128, 512], F32) for j in range(2)] for g in range(2)]
        osb = [[ar.alloc(f"osb{g}{j}", [128, 512], F32) for j in range(2)] for g in range(2)]
        ot = [ar.alloc(f"ot{i}", [128, 512], F32) for i in range(2)]
        osq = [ar.alloc(f"osq{i}", [128, 512], F32) for i in range(2)]
        assert ar.p <= p01_base
        eTB = [sy.buf(f"eT{i}") for i in range(6)]
        ssbB = [[sy.buf(f"ssb{g}{j}") for j in range(2)] for g in range(2)]
        osbB = [[sy.buf(f"osb{g}{j}") for j in range(2)] for g in range(2)]
        otB = [sy.buf(f"ot{i}") for i in range(2)]
        osqB = [sy.buf(f"osq{i}") for i in range(2)]
        if seq == 0:
            sy.dma("pool", k_c4, wupb_d[:, :], wup_d[:, :], writes=[wupbB])
            sy.dma("pool", k_c5, wdnb_d[:, :], wdn_d[:, :], writes=[wdnbB])
            sy.dma("sp", k_wout, wout[:], woutb_d.rearrange("(c p) n -> p c n", p=128), reads=[woutbB], writes=[woutB])
        ecnt = 0
        scnt = 0
        gcnt = 0
        O = [0, 1]
        Sm = [2, 3]
        fin_pending = []
        for h in range(4):
            for qt in range(4):
                nk = 4 * (qt + 1)
                q0 = qt * 512
                pend = []
                for kt in range(nk):
                    di = kt - 4 * qt
                    c0 = 128 * di if di > 0 else 0
                    cur = []
                    for j in range(2):
                        sbk = 4 + (scnt % 4)
                        scnt += 1
                        es_ = ecnt % 6
                        ecnt += 1
                        pr = slice(64 * j, 64 * j + 64)
                        T(lambda e: e.matmul(pb[sbk][:, c0:512], lhsT=kT[pr, h, kt * 128:(kt + 1) * 128],
                                             rhs=qT[pr, h, q0 + c0:q0 + 512], start=True, stop=True),
                          reads=[kB[h][kt // 4], qB[h][qt]], writes=[pbB[sbk]])
                        A(lambda e: e.activation(out=eT[es_][:, c0:512], in_=pb[sbk][:, c0:512], func=AF.Exp, scale=0.125),
                          reads=[pbB[sbk]], writes=[eTB[es_]])
                        if di >= 0:
                            P(lambda e: e.tensor_tensor(out=eT[es_][:, c0:c0 + 128], in0=eT[es_][:, c0:c0 + 128],
                                                        in1=tri[:], op=ALU.mult), reads=[smB], writes=[eTB[es_]])

                        def av(es_=es_, j=j, kt=kt, c0=c0, nk=nk, h=h):
                            T(lambda e: e.matmul(pb[O[j]][:, c0:512], lhsT=vt[:, kt, h * 128:(h + 1) * 128],
                                                 rhs=eT[es_][:, c0:512], start=(kt == 0), stop=(kt == nk - 1)),
                              reads=[eTB[es_], vB[kt]], writes=[pbB[O[j]]], inc=(kt == nk - 1))
                            T(lambda e: e.matmul(pb[Sm[j]][:, c0:512], lhsT=ones_b[:], rhs=eT[es_][:, c0:512],
                                                 start=(kt == 0), stop=(kt == nk - 1)),
                              reads=[eTB[es_], smB], writes=[pbB[Sm[j]]], inc=True)
                        cur.append(av)
                    if len(pend) >= 2:
                        for f in pend.pop(0):
                            f()
                    pend.append(cur)
                    if kt == min(5, nk - 1):
                        for fa, fb in fin_pending:
                            fa()
                    if kt == min(9, nk - 1):
                        for fa, fb in fin_pending:
                            fb()
                        fin_pending = []
                for grp in pend:
                    for f in grp:
                        f()
                pend = []
                g = gcnt % 2
                gcnt += 1
                V(lambda e: e.tensor_copy(out=osb[g][0][:], in_=pb[O[0]][:, :]), reads=[pbB[O[0]]], writes=[osbB[g][0]])
                V(lambda e: e.tensor_scalar_mul(out=ssb[g][0][:], in0=pb[Sm[0]][:, :], scalar1=2.0 ** -10), reads=[pbB[Sm[0]]], writes=[ssbB[g][0]])
                V(lambda e: e.tensor_copy(out=osb[g][1][:], in_=pb[O[1]][:, :]), reads=[pbB[O[1]]], writes=[osbB[g][1]])
                V(lambda e: e.tensor_scalar_mul(out=ssb[g][1][:], in0=pb[Sm[1]][:, :], scalar1=2.0 ** -10), reads=[pbB[Sm[1]]], writes=[ssbB[g][1]])
                V(lambda e: e.tensor_tensor(out=ot[g][:], in0=osb[g][0][:], in1=ssb[g][1][:], op=ALU.mult),
                  reads=[osbB[g][0], ssbB[g][1]], writes=[otB[g]])
                V(lambda e: e.tensor_tensor(out=osb[g][1][:], in0=osb[g][1][:], in1=ssb[g][0][:], op=ALU.mult),
                  reads=[ssbB[g][0]], writes=[osbB[g][1]])
                V(lambda e: e.scalar_tensor_tensor(out=ot[g][:], in0=osb[g][1][:], scalar=sm[:, 6:7], in1=ot[g][:],
                                                   op0=ALU.mult, op1=ALU.add), reads=[smB, osbB[g][1]], writes=[otB[g]])
                V(lambda e: e.tensor_tensor(out=ssb[g][0][:], in0=ssb[g][0][:], in1=ssb[g][1][:], op=ALU.mult),
                  reads=[ssbB[g][1]], writes=[ssbB[g][0]])
                V(lambda e: e.scalar_tensor_tensor(out=ssb[g][1][:], in0=ssb[g][0][:], scalar=1e-5, in1=ssb[g][0][:],
                                                   op0=ALU.mult, op1=ALU.mult), reads=[ssbB[g][0]], writes=[ssbB[g][1]])

                def fin_a(g=g):
                    nonlocal scnt
                    V(lambda e: e.tensor_tensor(out=osq[g][:], in0=ot[g][:], in1=ot[g][:], op=ALU.mult), reads=[otB[g]], writes=[osqB[g]])
                    sbk = 4 + (scnt % 4)
                    scnt += 1
                    T(lambda e: e.matmul(pb[sbk][:, :], lhsT=cmf[:, 3, :], rhs=osq[g][:], start=True, stop=True),
                      reads=[osqB[g], constB], writes=[pbB[sbk]])
                    V(lambda e: e.scalar_tensor_tensor(out=osq[g][:], in0=pb[sbk][:, :], scalar=1.0 / 128, in1=ssb[g][1][:],
                                                       op0=ALU.mult, op1=ALU.add), reads=[pbB[sbk], ssbB[g][1]], writes=[osqB[g]])

                def fin_b(g=g, h=h, qt=qt, q0=q0):
                    A(lambda e: e.activation(out=osq[g][:], in_=osq[g][:], func=AF.Ln), reads=[osqB[g]], writes=[osqB[g]])
                    A(lambda e: e.activation(out=osq[g][:], in_=osq[g][:], func=AF.Exp, scale=-0.5), writes=[osqB[g]])
                    V(lambda e: e.scalar_tensor_tensor(out=mixT[:, 4 + h, q0:q0 + 512], in0=ot[g][:], scalar=sm[:, 7:8],
                                                       in1=osq[g][:], op0=ALU.mult, op1=ALU.mult),
                      reads=[smB, osqB[g], otB[g]], writes=[mixB[4 + h][qt]])
                fin_pending.append((fin_a, fin_b))
                if gcnt <= 4:
                    sbk = 4 + (scnt % 4)
                    scnt += 1
                    ga_piece(gcnt - 1, sbk)
        for fa, fb in fin_pending:
            fa()
            fb()
        fin_pending = []

        if debug and seq == 0:
            sy.barrier()
            kd = sy.dma_sem("dbg")
            sy.dma("sp", kd, dbg_mix[:, :], mixT[:].rearrange("p c t -> p (c t)"))
            sy.dma("sp", kd, dbg_q[:, :], qT[:].rearrange("p c t -> p (c t)"))
            sy.dma("sp", kd, dbg_k[:, :], kT[:].rearrange("p c t -> p (c t)"))
            sy.dma("sp", kd, dbg_v[:, :], vt[:].rearrange("p c t -> p (c t)"))
        sy.barrier()
        ar.p = region
        wdn = ar.alloc("wdn", [128, NFF, 1024], BF16)
        g2bc = ar.alloc("g2bc", [128, 1024], F32)
        gfbc = ar.alloc("gfbc", [128, 1024], F32)
        x1 = ar.alloc("x1", [128, 2, 4, 1024], F32)
        h2T = ar.alloc("h2T", [128, 8, 512], BF16)
        actT = ar.alloc("actT", [128, NFF, 512], BF16)
        xr = [ar.alloc(f"xr{i}", [128, 1024], F32) for i in range(2)]
        hb2 = [ar.alloc(f"hb2{i}", [128, 1024], BF16) for i in range(2)]
        wus = [ar.alloc(f"wus{i}", [128, 8, 256], BF16) for i in range(3)]
        accg = [ar.alloc(f"accg{i}", [128, 512], F32) for i in range(2)]
        accu = [ar.alloc(f"accu{i}", [128, 512], F32) for i in range(2)]
        sgt = [ar.alloc(f"sgt{i}", [128, 512], F32) for i in range(2)]
        junk = ar.alloc("junk", [128, 1024], BF16)
        wdnB, g2B, gfB = sy.buf("wdn"), sy.buf("g2"), sy.buf("gf")
        x1B = [[sy.buf(f"x1{p}{i}") for i in range(4)] for p in range(2)]
        h2B = [sy.buf(f"h2{i}") for i in range(4)]
        actB = [sy.buf(f"act{i}") for i in range(NFF)]
        xrB = [sy.buf(f"xr{i}") for i in range(2)]
        hb2B = [sy.buf(f"hb2{i}") for i in range(2)]
        wusB = [sy.buf(f"wus{i}") for i in range(3)]
        accgB = [sy.buf(f"accg{i}") for i in range(2)]
        accuB = [sy.buf(f"accu{i}") for i in range(2)]
        sgB = [sy.buf(f"sg{i}") for i in range(2)]
        junkB = sy.buf("junk")
        k_f0 = sy.dma_sem(f"f0_{seq}")
        k_f1 = sy.dma_sem(f"f1_{seq}")
        k_f2 = sy.dma_sem(f"f2_{seq}")
        k_xr = [sy.dma_sem(f"xr{seq}_{i}") for i in range(2)]
        k_wus = [sy.dma_sem(f"wus{seq}_{i}") for i in range(3)]
        k_out = [sy.dma_sem(f"out{seq}_{i}") for i in range(4)]
        out_keys += k_out

        sy.dma("sp", k_f0, g2bc[:], g3_d[1:2, :].partition_broadcast(128), writes=[g2B])
        sy.dma("sp", k_f1, gfbc[:], g3_d[2:3, :].partition_broadcast(128), writes=[gfB])

        def load_xr(n):
            r0 = seq * S + n * 128
            sy.dma("sp", k_xr[n % 2], xr[n % 2][:], x_d[r0:r0 + 128, :], writes=[xrB[n % 2]])

        wu_n = {"n": 0}

        def load_wup(c):
            s3 = wu_n["n"] % 3
            wu_n["n"] += 1
            sy.dma("sp", k_wus[s3], wus[s3][:],
                   wupb_d[c * 256:(c + 1) * 256, :].rearrange("(p a) n -> p (a n)", a=2).rearrange("p (k c) -> p k c", k=8),
                   reads=[wupbB], writes=[wusB[s3]])
            return s3

        load_xr(0)
        load_xr(1)
        wdn_pieces = [(0, 6), (6, 12), (12, 17), (17, NFF)]
        wdnPB = [sy.buf(f"wdnp{i}") for i in range(4)]
        k_wdn = [sy.dma_sem(f"wdn{seq}_{i}") for i in range(4)]
        wdn_of = {}
        for pi_, (c0_, c1_) in enumerate(wdn_pieces):
            for c_ in range(c0_, c1_):
                wdn_of[c_] = pi_

        def load_wdn(pi_):
            c0_, c1_ = wdn_pieces[pi_]
            sy.dma("sp", k_wdn[pi_], wdn[:, c0_:c1_, :],
                   wdnb_d[c0_ * 128:c1_ * 128, :].rearrange("(c p) n -> p c n", p=128), reads=[wdnbB], writes=[wdnPB[pi_]])

        def p3_mm(tl, sub):
            n = tl * 4 + sub
            lc = n * 128
            par = tl % 2
            bp = (4, 5)
            for half in range(2):
                for c in range(8):
                    T(lambda e: e.matmul(pb[bp[half]][:, :], lhsT=mixT[:, c, lc:lc + 128],
                                         rhs=wout[:, c, half * 512:(half + 1) * 512], start=(c == 0), stop=(c == 7)),
                      reads=[mixB[c][tl], woutB], writes=[pbB[bp[half]]], inc=(c == 7))
                V(lambda e: e.tensor_tensor(out=x1[:, par, sub, half * 512:(half + 1) * 512], in0=pb[bp[half]][:, :],
                                            in1=xr[n % 2][:, half * 512:(half + 1) * 512], op=ALU.add),
                  reads=[pbB[bp[half]], xrB[n % 2]], writes=[x1B[par][sub]])
            if n + 2 < 16:
                load_xr(n + 2)
            xs = n % 2
            rstd, sb = rms_stat(2 + xs, x1[:, par, sub, :], junk[:], [x1B[par][sub]], junkB, D, 0)
            V(lambda e: e.scalar_tensor_tensor(out=hb2[xs][:], in0=x1[:, par, sub, :], scalar=rstd, in1=g2bc[:],
                                               op0=ALU.mult, op1=ALU.mult),
              reads=[x1B[par][sub], sb, g2B], writes=[hb2B[xs]], strict=[sb])

        def p3_tr(tl, sub):
            n = tl * 4 + sub
            xs = n % 2
            for kc in range(8):
                T(lambda e: e.transpose(out=pbT[:, kc * 128:(kc + 1) * 128], in_=hb2[xs][:, kc * 128:(kc + 1) * 128],
                                        identity=ident[:]),
                  reads=[hb2B[xs], smB], writes=[pbTB], inc=(kc == 7))
            A(lambda e: e.activation(out=h2T[:, :, sub * 128:(sub + 1) * 128],
                                     in_=pbT[:].rearrange("p (k c) -> p k c", k=8), func=AF.Copy),
              reads=[pbTB], writes=[h2B[sub]])

        def p3_pieces(tl):
            return [lambda: p3_mm(tl, 0), lambda: p3_mm(tl, 1), lambda: p3_tr(tl, 0), lambda: p3_mm(tl, 2),
                    lambda: p3_tr(tl, 1), lambda: p3_mm(tl, 3), lambda: p3_tr(tl, 2), lambda: p3_tr(tl, 3)]

        def ffn_up(tl):
            slots = [load_wup(0), load_wup(1)]
            for c in range(NFF):
                if c + 2 < NFF:
                    slots.append(load_wup(c + 2))
                if tl == 0 and c in (2, 6, 10, 14):
                    load_wdn((c - 2) // 4)
                s3 = slots[c]
                a2 = c % 2
                banks = (next_bank(6), next_bank(6))
                for gu in range(2):
                    bi = banks[gu]
                    for kc in range(8):
                        T(lambda e: e.matmul(pb[bi][:, :], lhsT=wus[s3][:, kc, gu * 128:(gu + 1) * 128], rhs=h2T[:, kc, :],
                                             start=(kc == 0), stop=(kc == 7)),
                          reads=[wusB[s3]] + h2B, writes=[pbB[bi]], inc=(kc == 7))
                for gu in range(2):
                    bi = banks[gu]
                    cc = c + gu * NFF
                    acc, accB_ = (accg[a2], accgB[a2]) if gu == 0 else (accu[a2], accuB[a2])
                    A(lambda e: e.activation(out=acc[:], in_=pb[bi][:, :], func=AF.Identity, scale=CW(2, cc), bias=CB(cc)),
                      reads=[pbB[bi], constB], writes=[accB_])
                    V(lambda e: e.scalar_tensor_tensor(out=acc[:, 1:512], in0=pb[bi][:, 0:511], scalar=CW(1, cc),
                                                       in1=acc[:, 1:512], op0=ALU.mult, op1=ALU.add),
                      reads=[pbB[bi]], writes=[accB_])
                    V(lambda e: e.scalar_tensor_tensor(out=acc[:, 2:512], in0=pb[bi][:, 0:510], scalar=CW(0, cc),
                                                       in1=acc[:, 2:512], op0=ALU.mult, op1=ALU.add),
                      reads=[pbB[bi]], writes=[accB_])
                    if tl > 0:
                        hp = (tl - 1) % 2
                        V(lambda e: e.scalar_tensor_tensor(out=acc[:, 0:1], in0=hal[:, hp, cc, 1:2], scalar=CW(1, cc),
                                                           in1=acc[:, 0:1], op0=ALU.mult, op1=ALU.add),
                          reads=[halB[hp][cc]], writes=[accB_])
                        V(lambda e: e.scalar_tensor_tensor(out=acc[:, 0:2], in0=hal[:, hp, cc, 0:2], scalar=CW(0, cc),
                                                           in1=acc[:, 0:2], op0=ALU.mult, op1=ALU.add),
                          reads=[halB[hp][cc]], writes=[accB_])
                    if tl < 3:
                        A(lambda e: e.activation(out=hal[:, tl % 2, cc, :], in_=pb[bi][:, 510:512], func=AF.Copy),
                          reads=[pbB[bi]], writes=[halB[tl % 2][cc]])
                A(lambda e: e.activation(out=sgt[a2][:], in_=accg[a2][:], func=AF.Silu), reads=[accgB[a2]], writes=[sgB[a2]])
                P(lambda e: e.tensor_tensor(out=actT[:, c, :], in0=sgt[a2][:], in1=accu[a2][:], op=ALU.mult),
                  reads=[sgB[a2], accuB[a2]], writes=[actB[c]])

        def ffn_down_half(tl, sub, half):
            n = tl * 4 + sub
            r0 = seq * S + n * 128
            par = tl % 2
            bp = (0, 1) if sub % 2 == 0 else (2, 3)
            for c in range(NFF):
                T(lambda e: e.matmul(pb[bp[half]][:, :], lhsT=actT[:, c, sub * 128:(sub + 1) * 128],
                                     rhs=wdn[:, c, half * 512:(half + 1) * 512], start=(c == 0), stop=(c == NFF - 1)),
                  reads=[actB[c], wdnPB[wdn_of[c]]], writes=[pbB[bp[half]]], inc=(c == NFF - 1))
            V(lambda e: e.tensor_tensor(out=x1[:, par, sub, half * 512:(half + 1) * 512], in0=pb[bp[half]][:, :],
                                        in1=x1[:, par, sub, half * 512:(half + 1) * 512], op=ALU.add),
              reads=[pbB[bp[half]]], writes=[x1B[par][sub]])
            if half == 1:
                rstd, sb = rms_stat(4 + (n % 2), x1[:, par, sub, :], junk[:], [x1B[par][sub]], junkB, D, 0)
                V(lambda e: e.scalar_tensor_tensor(out=x1[:, par, sub, :], in0=x1[:, par, sub, :], scalar=rstd, in1=gfbc[:],
                                                   op0=ALU.mult, op1=ALU.mult),
                  reads=[sb, gfB], writes=[x1B[par][sub]], strict=[sb])
                sy.dma("pool", k_out[sub], out_d[r0:r0 + 128, :], x1[:, par, sub, :], reads=[x1B[par][sub]])

        for f in p3_pieces(0):
            f()
        for tl in range(4):
            ffn_up(tl)
            pieces = p3_pieces(tl + 1) if tl < 3 else []
            pi = 0
            for sub in range(4):
                for half in range(2):
                    ffn_down_half(tl, sub, half)
                    if pi < len(pieces):
                        pieces[pi]()
                        pi += 1

    sy.final_wait("pool", out_keys)
    sy.barrier()
    es.close()
    return nc


_NC = None


def _host_consts():
    inv = (10000.0 ** (-(np.arange(0, 64, 2, dtype=np.float32) / np.float32(64)))).astype(np.float32)
    ang = (np.arange(S, dtype=np.float32)[:, None] * inv[None, :]).astype(np.float32)
    cos = np.cos(ang).astype(np.float32).T
    sin = np.sin(ang).astype(np.float32).T
    cs = np.zeros((2, 128, S), np.float32)
    for p in range(128):
        d = p % 64
        i = d % 32
        cs[0, p] = cos[i]
        cs[1, p] = -sin[i] if d < 32 else sin[i]
    cm = np.zeros((4, 128, 128), np.float32)
    cm[0] = np.eye(128, dtype=np.float32)
    for p in range(128):
        d = p % 64
        partner = p + 32 if d < 32 else p - 32
        cm[1, partner, p] = 1.0
    kk = np.arange(128)[:, None]
    qq = np.arange(128)[None, :]
    cm[2] = (kk <= qq).astype(np.float32)
    cm[3] = 1.0
    return cs.reshape(256, S), cm.reshape(512, 128)


def kernel(x, norm1_g, w_in, short_conv_w, mix_a_norm_g, lambda_q1, lambda_k1, lambda_q2, lambda_k2,
           subln_g, w_out, norm2_g, w_up, ffn_conv_w, ffn_conv_b, w_down, final_g):
    global _NC
    f = lambda a: np.ascontiguousarray(np.asarray(a, dtype=np.float32))
    x = f(x)
    w_in0 = f(w_in)[0]
    order = []
    for i in range(4):
        order += [512 + 128 * i, 1024 + 128 * i, 128 * i]
    order += [1536 + 128 * h for h in range(4)] + [2048 + 128 * h for h in range(4)]
    w_in_fm = np.stack([w_in0[:, c:c + 128].reshape(8, 128, 128).transpose(1, 0, 2) for c in order])
    w_in_fm = np.ascontiguousarray(w_in_fm).reshape(20 * 128, 1024)
    w_v = np.ascontiguousarray(w_in0[:, 2560:3072])
    w_up0 = f(w_up)[0]
    chunks = []
    for c in range(NFF):
        gcol = w_up0[:, c * 128:(c + 1) * 128].reshape(8, 128, 128)
        ucol = w_up0[:, DFF + c * 128:DFF + (c + 1) * 128].reshape(8, 128, 128)
        chunks.append(np.concatenate([gcol, ucol], axis=2).transpose(1, 0, 2))
    w_up_fm = np.ascontiguousarray(np.stack(chunks)).reshape(NFF * 128 * 2, 1024)
    g3 = np.ascontiguousarray(np.stack([f(norm1_g)[0], f(norm2_g)[0], f(final_g)]))
    lamv = np.concatenate([f(lambda_q1)[0], f(lambda_k1)[0], f(lambda_q2)[0], f(lambda_k2)[0]])[None, :]
    cols = np.zeros((128, NCOLS), np.float32)
    fw = f(ffn_conv_w)[0]
    for j in range(3):
        cols[:, j * 44:(j + 1) * 44] = fw[j].reshape(44, 128).T
    cols[:, 132:176] = f(ffn_conv_b)[0].reshape(44, 128).T
    sw = f(short_conv_w)[0]
    for j in range(3):
        cols[:, 176 + 4 * j:180 + 4 * j] = sw[j].reshape(4, 128).T
    cols[:, 188:192] = f(mix_a_norm_g)[0].reshape(4, 128).T
    cols[:, 192] = f(subln_g)[0]
    cs, cm = _host_consts()
    shared = {
        "w_in_fm": w_in_fm, "w_v": w_v, "w_out": f(w_out)[0], "w_up_fm": w_up_fm, "w_down": f(w_down)[0],
        "g3": g3, "lamv": np.ascontiguousarray(lamv), "cols": cols, "cs": cs, "cmat": cm,
    }
    xs = x.reshape(NCORES, TOK, D)
    in_maps = [dict(shared, x=np.ascontiguousarray(xs[c])) for c in range(NCORES)]
    if _NC is None:
        _NC = build_nc()
    res = run_bass_kernel_spmd(_NC, in_maps, core_ids=list(range(NCORES)))
    out = np.stack([np.asarray(r["out"]) for r in res.results]).reshape(16, S, D)
    return out.astype(np.float32)
```
